# Optimizing a Trainium2 kernel written in Bass

```python
import jax, jax.numpy as jnp
from jax import lax
import numpy as np

D_MODEL = 2048
BATCH = 4
SEQ = 4096
DEPTH = 1

N_Q_HEADS = 16
N_KV_HEADS = 4
HEAD_DIM = 64
Q_GROUP = N_Q_HEADS // N_KV_HEADS
ATTN_WIDTH = N_Q_HEADS * HEAD_DIM
KV_WIDTH = N_KV_HEADS * HEAD_DIM
WINDOW = 128
BLOCK = 128
ROPE_THETA = 500000.0
ROT_DIM = HEAD_DIM // 4
POOL_WINDOWS = (2, 4, 8, 16)
N_POOL_GROUPS = len(POOL_WINDOWS)
POOL_WIDTH = D_MODEL // 2
POOL_GROUP = POOL_WIDTH // N_POOL_GROUPS
N_BRANCHES = 2
IN_SPLITS = (POOL_WIDTH, ATTN_WIDTH, KV_WIDTH, KV_WIDTH, D_MODEL, D_MODEL)
IN_WIDTH = sum(IN_SPLITS)
D_FF = 5504
N_SUBLAYERS = 3
LN_EPS = 1e-5
DN_ALPHA = (2 * DEPTH) ** 0.25
DN_BETA = (8 * DEPTH) ** -0.25

kernel_name = "hybrid_pool_swa_macaron_deepnorm_adaln"


def layer_norm(x, g, b):
    xf = x.astype(jnp.float32)
    mu = jnp.mean(xf, axis=-1, keepdims=True)
    var = jnp.mean(jnp.square(xf - mu), axis=-1, keepdims=True)
    y = (xf - mu) * lax.rsqrt(var + LN_EPS)
    return (y * g.astype(jnp.float32) + b.astype(jnp.float32)).astype(x.dtype)


def modulate(x, shift, scale):
    return x * (1.0 + scale[:, None, :]) + shift[:, None, :]


def swiglu(u, w_gu, w_down):
    a, b = jnp.split(u @ w_gu, 2, axis=-1)
    return (jax.nn.silu(a) * b) @ w_down


def rope_partial(t, cos, sin):
    half = ROT_DIM // 2
    t1 = t[..., :half]
    t2 = t[..., half:ROT_DIM]
    c = cos[None, :, None, :].astype(t.dtype)
    s = sin[None, :, None, :].astype(t.dtype)
    return jnp.concatenate([t1 * c - t2 * s, t2 * c + t1 * s, t[..., ROT_DIM:]], axis=-1)


def pool_mixer(xp, w_pool, pool_scale):
    B, S, _ = xp.shape
    groups = xp.reshape(B, S, N_POOL_GROUPS, POOL_GROUP)
    t1 = jnp.arange(S) + 1
    outs = []
    for gi, w in enumerate(POOL_WINDOWS):
        xg = groups[:, :, gi, :].astype(jnp.float32)
        cs = jnp.cumsum(xg, axis=1)
        lag = jnp.pad(cs, ((0, 0), (w, 0), (0, 0)))[:, :S]
        count = jnp.minimum(t1, w).astype(jnp.float32)[None, :, None]
        outs.append((cs - lag) / count - xg)
    pooled = jnp.stack(outs, axis=2).astype(xp.dtype)
    mixed = jnp.einsum('bsgc,gcd->bsgd', pooled, w_pool)
    return mixed.reshape(B, S, POOL_WIDTH) * pool_scale


def sliding_window_attention(q, k, v, sinks):
    B, S = q.shape[0], q.shape[1]
    nb = S // BLOCK
    qb = q.reshape(B, nb, BLOCK, N_KV_HEADS, Q_GROUP, HEAD_DIM)

    def with_prev(t):
        tb = t.reshape(B, nb, BLOCK, N_KV_HEADS, HEAD_DIM)
        prev = jnp.pad(tb[:, :-1], ((0, 0), (1, 0), (0, 0), (0, 0), (0, 0)))
        return jnp.concatenate([prev, tb], axis=2)

    kw = with_prev(k)
    vw = with_prev(v)
    s = jnp.einsum('bnqhgd,bnkhd->bnhgqk', qb, kw,
                   preferred_element_type=jnp.float32) * (HEAD_DIM ** -0.5)
    qi = jnp.arange(BLOCK)[:, None]
    kj = jnp.arange(2 * BLOCK)[None, :]
    diff = qi - kj + BLOCK
    kpos = jnp.arange(nb)[:, None, None] * BLOCK - BLOCK + kj[None]
    valid = (diff >= 0)[None] & (diff < WINDOW)[None] & (kpos >= 0)
    s = jnp.where(valid[None, :, None, None], s, -1e30)
    sink = sinks.astype(jnp.float32).reshape(1, 1, N_KV_HEADS, Q_GROUP, 1, 1)
    m = jnp.maximum(jnp.max(s, axis=-1, keepdims=True), sink)
    p = jnp.exp(s - m)
    probs = p / (jnp.sum(p, axis=-1, keepdims=True) + jnp.exp(sink - m))
    o = jnp.einsum('bnhgqk,bnkhd->bnqhgd', probs.astype(v.dtype), vw)
    return o.reshape(B, S, ATTN_WIDTH)


def setup_inputs(seed: int = 0) -> dict:
    key = jax.random.key(seed)
    ks = jax.random.split(key, 24)
    f32 = jnp.float32
    L, D = DEPTH, D_MODEL

    def nrm(k, shape, std):
        return jax.random.normal(k, shape, f32) * std

    x = jax.random.normal(ks[0], (BATCH, SEQ, D), f32)
    c = jax.random.normal(ks[1], (BATCH, D), f32)
    w_ada = nrm(ks[2], (L, D, N_SUBLAYERS * 3 * D), 0.2 * D ** -0.5)
    b_ada = nrm(ks[3], (L, N_SUBLAYERS * 3 * D), 0.01)
    ln_g = 1.0 + nrm(ks[4], (L, N_SUBLAYERS, D), 0.05)
    ln_b = nrm(ks[5], (L, N_SUBLAYERS, D), 0.01)
    w_ffn1_in = nrm(ks[6], (L, D, 2 * D_FF), DN_BETA * D ** -0.5)
    w_ffn1_out = nrm(ks[7], (L, D_FF, D), DN_BETA * D_FF ** -0.5)
    w_in = jnp.concatenate([
        nrm(ks[8], (L, D, POOL_WIDTH), D ** -0.5),
        nrm(ks[9], (L, D, ATTN_WIDTH), D ** -0.5),
        nrm(ks[10], (L, D, KV_WIDTH), D ** -0.5),
        nrm(ks[11], (L, D, KV_WIDTH), DN_BETA * D ** -0.5),
        nrm(ks[12], (L, D, N_BRANCHES * D), D ** -0.5),
    ], axis=-1)
    b_in = nrm(ks[13], (L, IN_WIDTH), 0.01)
    w_pool = nrm(ks[14], (L, N_POOL_GROUPS, POOL_GROUP, POOL_GROUP), POOL_GROUP ** -0.5)
    pool_scale = 1.0 + nrm(ks[15], (L, POOL_WIDTH), 0.1)
    sinks = nrm(ks[16], (L, N_Q_HEADS), 0.5)
    w_branch_a = nrm(ks[17], (L, POOL_WIDTH, D), DN_BETA * POOL_WIDTH ** -0.5)
    w_branch_b = nrm(ks[18], (L, ATTN_WIDTH, D), DN_BETA * ATTN_WIDTH ** -0.5)
    w_out = nrm(ks[19], (L, D, D), DN_BETA * D ** -0.5)
    w_ffn2_in = nrm(ks[20], (L, D, 2 * D_FF), DN_BETA * D ** -0.5)
    w_ffn2_out = nrm(ks[21], (L, D_FF, D), DN_BETA * D_FF ** -0.5)
    return {"x": x, "c": c, "w_ada": w_ada, "b_ada": b_ada, "ln_g": ln_g, "ln_b": ln_b,
            "w_ffn1_in": w_ffn1_in, "w_ffn1_out": w_ffn1_out, "w_in": w_in, "b_in": b_in,
            "w_pool": w_pool, "pool_scale": pool_scale, "sinks": sinks,
            "w_branch_a": w_branch_a, "w_branch_b": w_branch_b, "w_out": w_out,
            "w_ffn2_in": w_ffn2_in, "w_ffn2_out": w_ffn2_out}


def reference(x, c, w_ada, b_ada, ln_g, ln_b, w_ffn1_in, w_ffn1_out, w_in, b_in, w_pool,
              pool_scale, sinks, w_branch_a, w_branch_b, w_out, w_ffn2_in, w_ffn2_out):
    B, S, D = x.shape
    pos = jnp.arange(S, dtype=jnp.float32)
    inv_freq = ROPE_THETA ** (-jnp.arange(0, ROT_DIM, 2, dtype=jnp.float32) / ROT_DIM)
    ang = pos[:, None] * inv_freq[None, :]
    cos, sin = jnp.cos(ang), jnp.sin(ang)
    split_at = list(np.cumsum(IN_SPLITS)[:-1])
    c_act = jax.nn.silu(c)

    for l in range(DEPTH):
        mod = (c_act @ w_ada[l] + b_ada[l]).reshape(B, N_SUBLAYERS, 3, D)

        u = modulate(x, mod[:, 0, 0], mod[:, 0, 1])
        y = swiglu(u, w_ffn1_in[l], w_ffn1_out[l])
        x = layer_norm(DN_ALPHA * x + 0.5 * (1.0 + mod[:, 0, 2])[:, None, :] * y,
                       ln_g[l, 0], ln_b[l, 0])

        u = modulate(x, mod[:, 1, 0], mod[:, 1, 1])
        h = u @ w_in[l] + b_in[l]
        xp, q, k, v, gl_a, gl_b = jnp.split(h, split_at, axis=-1)
        q = rope_partial(q.reshape(B, S, N_Q_HEADS, HEAD_DIM), cos, sin)
        k = rope_partial(k.reshape(B, S, N_KV_HEADS, HEAD_DIM), cos, sin)
        v = v.reshape(B, S, N_KV_HEADS, HEAD_DIM)
        y_a = pool_mixer(xp, w_pool[l], pool_scale[l]) @ w_branch_a[l]
        y_b = sliding_window_attention(q, k, v, sinks[l]) @ w_branch_b[l]
        merged = jax.nn.sigmoid(gl_a) * y_a + jax.nn.sigmoid(gl_b) * y_b
        y = merged @ w_out[l]
        x = layer_norm(DN_ALPHA * x + (1.0 + mod[:, 1, 2])[:, None, :] * y,
                       ln_g[l, 1], ln_b[l, 1])

        u = modulate(x, mod[:, 2, 0], mod[:, 2, 1])
        y = swiglu(u, w_ffn2_in[l], w_ffn2_out[l])
        x = layer_norm(DN_ALPHA * x + 0.5 * (1.0 + mod[:, 2, 2])[:, None, :] * y,
                       ln_g[l, 2], ln_b[l, 2])
    return x
```

```python
import contextlib
import numpy as np
import concourse.bass as bass
import concourse.mybir as mybir
from concourse.bass_utils import run_bass_kernel_spmd

F32 = mybir.dt.float32
BF16 = mybir.dt.bfloat16
AF = mybir.ActivationFunctionType
ALU = mybir.AluOpType

D = 2048
KC = 16
DFF = 5504
NFC = 43
TP = 1152
OWN = 1024
ALPHA = 2.0 ** 0.25
LN_EPS = 1e-5
NEG = -240000.0
NWORDS = 53100
DEBUG = False
POOL_ACC = False

ENGS = ("pe", "act", "dve", "pool", "sp")


class Op:
    __slots__ = ("eng", "fn", "deps", "dma_key", "dma_n", "signal", "sigval", "idx", "cdeps", "ddeps")

    def __init__(self, eng, fn, dma_key):
        self.eng = eng
        self.fn = fn
        self.deps = []
        self.dma_key = dma_key
        self.dma_n = 0
        self.signal = False
        self.sigval = 0


class Sched:
    def __init__(self, nc):
        self.nc = nc
        self.ops = {e: [] for e in ENGS}
        self.last_writer = {}
        self.readers = {}
        self.dma_count = {}
        self.pending = {e: [] for e in ENGS}
        self.fam_last = {}
        self.fam_pending = {}

    def barrier(self):
        last = {e: (self.ops[e][-1] if self.ops[e] else None) for e in ENGS}
        for e in ENGS:
            for e2 in ENGS:
                if e2 != e and last[e2] is not None:
                    self.pending[e].append(last[e2])

    def op(self, eng, fn, reads=(), writes=(), dma_key=None):
        o = Op(eng, fn, dma_key)
        is_dma = dma_key is not None
        deps = {}
        for t in reads:
            w = self.last_writer.get(t)
            if w is not None:
                deps[id(w)] = (w, "raw")
        for t in writes:
            w = self.last_writer.get(t)
            if w is not None and id(w) not in deps:
                deps[id(w)] = (w, "waw")
            for r in self.readers.get(t, ()):
                if id(r) not in deps:
                    deps[id(r)] = (r, "war")
        for p, kind in deps.values():
            p_dma = p.dma_key is not None
            if not p_dma and not is_dma and p.eng == eng:
                if eng == "pe" or kind != "raw":
                    continue
            o.deps.append(p)
        if self.pending[eng]:
            o.deps.extend(self.pending[eng])
            self.pending[eng] = []
        lkey = ("dma", dma_key) if is_dma else eng
        for t in tuple(reads) + tuple(writes):
            fam = t[0] if isinstance(t, tuple) else t
            fp = self.fam_pending.get(fam)
            if fp is not None and eng not in fp[1]:
                fp[1].add(eng)
                o.deps.extend(fp[0])
            self.fam_last.setdefault(fam, {})[lkey] = o
        for t in writes:
            self.last_writer[t] = o
            self.readers[t] = []
        for t in reads:
            self.readers.setdefault(t, []).append(o)
        if is_dma:
            n = self.dma_count.get(dma_key, 0) + 1
            self.dma_count[dma_key] = n
            o.dma_n = n
        o.idx = len(self.ops[eng])
        self.ops[eng].append(o)
        return o

    def emit(self, final_waits=()):
        nc = self.nc
        for e in ENGS:
            for o in self.ops[e]:
                cd, dd = {}, {}
                for p in o.deps:
                    if p.dma_key is None:
                        q = cd.get(p.eng)
                        if q is None or p.idx > q.idx:
                            cd[p.eng] = p
                    else:
                        q = dd.get(p.dma_key)
                        if q is None or p.dma_n > q.dma_n:
                            dd[p.dma_key] = p
                o.cdeps, o.ddeps = cd, dd
                for p in cd.values():
                    p.signal = True
        for o in final_waits:
            if o.dma_key is None:
                o.signal = True
        for e in ENGS:
            c = 0
            for o in self.ops[e]:
                if o.dma_key is None and o.signal:
                    c += 1
                    o.sigval = c
        with contextlib.ExitStack() as st:
            esem = {e: st.enter_context(nc.semaphore("s_" + e)) for e in ENGS}
            dsem = {}
            for i, k in enumerate(self.dma_count):
                dsem[k] = st.enter_context(nc.semaphore("d%d" % i))
            print("sched: ops", {e: len(self.ops[e]) for e in ENGS}, "dma keys", len(self.dma_count))
            block = st.enter_context(nc.Block())

            def run(e, eng):
                known = {}
                for o in self.ops[e]:
                    need = {}
                    for p in o.ddeps.values():
                        s = dsem[p.dma_key]
                        need[id(s)] = (s, 16 * p.dma_n)
                    for p in o.cdeps.values():
                        s = esem[p.eng]
                        need[id(s)] = (s, p.sigval)
                    for key, (s, v) in need.items():
                        if known.get(key, 0) >= v:
                            continue
                        eng.wait_ge(s, v)
                        known[key] = v
                    ins = o.fn(eng)
                    if o.dma_key is not None:
                        ins.then_inc(dsem[o.dma_key], 16)
                    elif o.signal:
                        ins.then_inc(esem[e], 1)
                if e == "sp":
                    for o in final_waits:
                        if o.dma_key is not None:
                            eng.wait_ge(dsem[o.dma_key], 16 * o.dma_n)
                        else:
                            eng.wait_ge(esem[o.eng], o.sigval)

            @block.tensor
            def _(eng):
                run("pe", eng)

            @block.scalar
            def _(eng):
                run("act", eng)

            @block.vector
            def _(eng):
                run("dve", eng)

            @block.gpsimd
            def _(eng):
                run("pool", eng)

            @block.sync
            def _(eng):
                run("sp", eng)


class Arena:
    def __init__(self, t, sched):
        self.t = t
        self.S = sched
        self.top = 0
        self.marks = []
        self.live = []
        self.dead = []
        self.hi = NWORDS
        self.live_top = []

    def _alloc(self, fam, words):
        w8 = (words + 7) // 8 * 8
        off = self.top
        self.top += w8
        assert self.top <= self.hi, ("arena overflow", fam, self.top, self.hi)
        ops = []
        for (f0, o0, s0, snap) in self.dead:
            if o0 < off + w8 and off < o0 + s0:
                ops.extend(snap)
        if ops:
            prev = self.S.fam_pending.get(fam)
            if prev is not None:
                ops = list(prev[0]) + ops
            uniq = {id(o): o for o in ops}
            self.S.fam_pending[fam] = (list(uniq.values()), set())
        self.live.append((fam, off, w8))
        return off

    def top_bf16(self, fam, n):
        w = (n + 1) // 2
        w8 = (w + 7) // 8 * 8
        self.hi -= w8
        off = self.hi
        assert self.top <= self.hi, ("arena overflow (top)", fam)
        ops = []
        for (f0, o0, s0, snap) in self.dead:
            if o0 < off + w8 and off < o0 + s0:
                ops.extend(snap)
        if ops:
            prev = self.S.fam_pending.get(fam)
            if prev is not None:
                ops = list(prev[0]) + ops
            uniq = {id(o): o for o in ops}
            self.S.fam_pending[fam] = (list(uniq.values()), set())
        self.live_top.append((fam, off, w8))
        return self.t[:, off:off + w].bitcast(BF16)

    def release_top(self):
        for (fam, off, sz) in self.live_top:
            snap = list(self.S.fam_last.get(fam, {}).values())
            self.dead.append((fam, off, sz, snap))
        self.live_top = []
        self.hi = NWORDS

    def f32(self, fam, n):
        off = self._alloc(fam, n)
        return self.t[:, off:off + n]

    def bf16(self, fam, n):
        w = (n + 1) // 2
        off = self._alloc(fam, w)
        return self.t[:, off:off + w].bitcast(BF16)

    def pad_to(self, off):
        assert off >= self.top, ("pad_to backwards", off, self.top)
        self.top = off

    def sub(self, words):
        w8 = (words + 7) // 8 * 8
        c = Arena(self.t, self.S)
        c.base = self.top
        c.top = self.top
        c.hi = self.top + w8
        c.dead = self.dead
        self.top += w8
        assert self.top <= self.hi
        return c

    def reset(self):
        for (fam, off, sz) in self.live:
            snap = list(self.S.fam_last.get(fam, {}).values())
            self.dead.append((fam, off, sz, snap))
        self.live = []
        self.top = self.base

    def mark(self):
        self.marks.append((self.top, len(self.live)))

    def release(self):
        top, nl = self.marks.pop()
        for (fam, off, sz) in self.live[nl:]:
            snap = list(self.S.fam_last.get(fam, {}).values())
            self.dead.append((fam, off, sz, snap))
        del self.live[nl:]
        self.top = top


def MM(out, lhsT, rhs, start, stop):
    return lambda e: e.matmul(out, lhsT, rhs, start=start, stop=stop)


def ACT(out, in_, func, bias=None, scale=None):
    kw = {}
    if bias is not None:
        kw["bias"] = bias
    if scale is not None:
        kw["scale"] = scale
    return lambda e: e.activation(out, in_, func, **kw)


def TT(out, a, b, op):
    return lambda e: e.tensor_tensor(out, a, b, op)


def TS(out, a, s1, s2, op0, op1):
    return lambda e: e.tensor_scalar(out, a, s1, s2, op0, op1)


def TS1(out, a, s1, op0):
    return lambda e: e.tensor_scalar(out, a, s1, None, op0)


def STT(out, a, s, b, op0, op1):
    return lambda e: e.scalar_tensor_tensor(out, a, s, b, op0, op1)


def CP(out, a):
    return lambda e: e.tensor_copy(out, a)


def RCP(out, a):
    return lambda e: e.reciprocal(out, a)


def DMA(out, in_):
    return lambda e: e.dma_start(out=out, in_=in_)


def MEMSET(out, v):
    return lambda e: e.memset(out, v)


def build_program():
    nc = bass.Bass("TRN2", target_bir_lowering=False)

    def din(name, shape):
        return nc.dram_tensor(name, list(shape), F32, kind="ExternalInput").ap()

    xT = din("xT", [D, 2176])
    cfm_d = din("cfm", [128, 16])
    wada_d = din("wada", [144, 128, 4 * 512])
    bada_d = din("bada", [128, 144])
    lngb_d = din("lngb", [128, 96])
    wgu_d = [din("wgu1", [NFC, 128, 16 * 256]), din("wgu2", [NFC, 128, 16 * 256])]
    wd_d = [din("wd1", [NFC, 128, D]), din("wd2", [NFC, 128, D])]
    win_d = din("win", [18, 128, 16 * 128])
    wv_d = din("wv", [128, 16 * 256])
    wmix_d = din("wmix", [16, 3, 128, 16 * 128])
    wo_d = din("wo", [16, 128, 16 * 128])
    wpool_d = din("wpool", [128, 4 * 2 * 256])
    bias_d = din("biasfm", [128, 64])
    bv_d = din("bvbc", [128, 256])
    sink_d = din("sinkbc", [128, 16])
    ropeC_d = din("ropeC", [128, 2176])
    ropeS_d = din("ropeS", [128, 2176])
    cmat_d = din("cmat", [128, 7 * 128])
    invc_d = din("invcnt", [128, 64])
    hv_d = din("hv", [128, 1])
    outT = nc.dram_tensor("outT", [D, 2048], F32, kind="ExternalOutput").ap()
    if DEBUG:
        dbg1 = nc.dram_tensor("dbg1", [D, 2048], F32, kind="ExternalOutput").ap()
        dbg2 = nc.dram_tensor("dbg2", [D, 2048], F32, kind="ExternalOutput").ap()

    S = Sched(nc)
    with contextlib.ExitStack() as st:
        arena_t = st.enter_context(nc.sbuf_tensor("arena", [128, NWORDS], F32))
        banks = [st.enter_context(nc.psum_tensor("bank%d" % i, [128, 512], F32)) for i in range(8)]
        A = Arena(arena_t, S)
        PB = lambda i: ("ps", i)

        xs = A.f32("x", KC * TP).rearrange("p (c t) -> p c t", c=KC)
        us = A.bf16("u", KC * TP).rearrange("p (c t) -> p c t", c=KC)
        cm = A.bf16("cm", 7 * 128).rearrange("p (m j) -> p m j", m=7)
        IDENT, MASKP, MASKD, MASKP0, PERM, ONESD, ONES = range(7)
        cfm = A.f32("cfm", 16)
        cact = A.bf16("cact", 16)
        modfm = A.f32("modfm", 144)
        badafm = A.f32("bada", 144)
        lngb = A.f32("lngb", 96)
        der = A.f32("der", 48 + 48 + 96)
        der2 = A.f32("der2", 64)
        modraw = A.f32("modraw", 144)
        biasfm = A.f32("biasfm", 64)
        bvbc = A.f32("bvbc", 256)
        nsink = A.f32("nsink", 16)
        invc = A.f32("invc", 64)
        hv = A.f32("hv", 1)
        onebf = A.bf16("onebf", 2)
        pcarry = A.f32("pcarry", 8 * 16).rearrange("p (c t) -> p c t", c=8)
        kcarry = A.bf16("kcarry", 2 * 128).rearrange("p (c t) -> p c t", c=2)
        vcarry = A.bf16("vcarry", 256)

        def s1p(s, c):
            return der[:, s * 16 + c:s * 16 + c + 1]

        def gt(s, c):
            return der[:, 48 + s * 16 + c:48 + s * 16 + c + 1]

        def lga(s, c):
            return der[:, 96 + s * 16 + c:96 + s * 16 + c + 1]

        def lba(s, c):
            return der[:, 144 + s * 16 + c:144 + s * 16 + c + 1]

        def shift(s, c):
            return modfm[:, (3 * s) * 16 + c:(3 * s) * 16 + c + 1]

        S.op("pool", DMA(cm.rearrange("p m j -> p (m j)"), cmat_d), writes=["cm"], dma_key="cm")
        for nm, dst, src in (("cfm", cfm, cfm_d), ("bada", badafm, bada_d), ("lngb", lngb, lngb_d),
                             ("biasfm", biasfm, bias_d), ("bvbc", bvbc, bv_d), ("nsink", nsink, sink_d),
                             ("invc", invc, invc_d), ("hv", hv, hv_d)):
            S.op("sp", DMA(dst, src), writes=[nm], dma_key=nm)
        S.op("dve", TS1(nsink, nsink, -1.0, ALU.mult), reads=["nsink"], writes=["nsink"])
        S.op("dve", MEMSET(onebf, 1.0), writes=["onebf"])
        S.op("act", ACT(cact, cfm, AF.Silu), reads=["cfm"], writes=["cact"])

        wsub = (0.5, 1.0, 0.5)
        for s_ in range(3):
            a_ = ALPHA if s_ < 2 else 1.0
            S.op("dve", TS1(der[:, 96 + s_ * 16:96 + (s_ + 1) * 16], lngb[:, s_ * 16:(s_ + 1) * 16], a_, ALU.mult),
                 reads=["lngb"], writes=[("der", "ln")])
            S.op("dve", TS1(der[:, 144 + s_ * 16:144 + (s_ + 1) * 16], lngb[:, 48 + s_ * 16:48 + (s_ + 1) * 16], a_, ALU.mult),
                 reads=["lngb"], writes=[("der", "ln")])

        def mod_alloc(ar):
            wada = [ar.bf16("wada", 4 * 512).rearrange("p (k j) -> p k j", k=4) for _ in range(2)]
            rowf = ar.f32("rowf", 512)
            rowh = ar.bf16("rowh", 512)
            rowl = ar.bf16("rowl", 512)
            return (wada, rowf, rowh, rowl)

        def mod_gen(nlist, bufs, rowbanks, tbank):
            wada, rowf, rowh, rowl = bufs
            tiles = [(n, q) for n in nlist for q in range(4)]

            def dma(i):
                n, q = tiles[i]
                t = n * 4 + q
                S.op("pool", DMA(wada[t % 2].rearrange("p k j -> p (k j)"), wada_d[t]), writes=[("wada", t % 2)],
                     dma_key=("wada", t % 2))

            dma(0)
            pend = [None]
            for i, (n, q) in enumerate(tiles):
                t = n * 4 + q
                sl = t % 2
                rb = rowbanks[n % 2]
                for k4 in range(4):
                    kc = q * 4 + k4
                    S.op("pe", MM(banks[rb][0:1, :], cact[:, kc:kc + 1], wada[sl][:, k4, :], kc == 0, kc == 15),
                         reads=[("wada", sl), "cact"], writes=[PB(rb)])
                if i + 1 < len(tiles):
                    dma(i + 1)
                if pend[0] is not None:
                    pend[0]()
                    pend[0] = None
                if q == 3:
                    S.op("act", ACT(rowf[0:1, :], banks[rb][0:1, :], AF.Identity), reads=[PB(rb)], writes=["rowf"])
                    S.op("dve", CP(rowh[0:1, :], rowf[0:1, :]), reads=["rowf"], writes=["rowh"])
                    S.op("dve", TT(rowl[0:1, :], rowf[0:1, :], rowh[0:1, :], ALU.subtract), reads=["rowf", "rowh"], writes=["rowl"])

                    def fin(n=n):
                        for j in range(4):
                            S.op("pe", MM(banks[tbank][:, j:j + 1], rowh[0:1, j * 128:(j + 1) * 128], onebf[0:1, 0:1], True, False),
                                 reads=["rowh", "onebf"], writes=[PB(tbank)])
                            S.op("pe", MM(banks[tbank][:, j:j + 1], rowl[0:1, j * 128:(j + 1) * 128], onebf[0:1, 0:1], False, True),
                                 reads=["rowl", "onebf"], writes=[PB(tbank)])
                        S.op("dve", CP(modraw[:, n * 4:n * 4 + 4], banks[tbank][:, 0:4]), reads=[PB(tbank)], writes=[("modraw", n)])
                        if n % 4 == 3:
                            v = n // 4
                            s_ = v // 3
                            vs = slice(v * 16, (v + 1) * 16)
                            S.op("dve", TT(modfm[:, vs], modraw[:, vs], badafm[:, vs], ALU.add),
                                 reads=[("modraw", n_) for n_ in range(n - 3, n + 1)] + ["bada"], writes=[("modfm", v)])
                            if v % 3 == 1:
                                S.op("dve", TS(der[:, s_ * 16:(s_ + 1) * 16], modfm[:, vs], 1.0, (1.0 if s_ == 0 else 1.0 / ALPHA), ALU.add, ALU.mult),
                                     reads=[("modfm", v)], writes=[("der", "s1p", s_)])
                                if s_ >= 1:
                                    sp_ = s_ - 1
                                    g2 = der2[:, sp_ * 16:(sp_ + 1) * 16]
                                    b2 = der2[:, 32 + sp_ * 16:32 + (sp_ + 1) * 16]
                                    S.op("dve", TT(g2, der[:, 96 + sp_ * 16:96 + (sp_ + 1) * 16], der[:, s_ * 16:(s_ + 1) * 16], ALU.mult),
                                         reads=[("der", "ln"), ("der", "s1p", s_)], writes=[("der2", sp_)])
                                    S.op("dve", TT(b2, der[:, 144 + sp_ * 16:144 + (sp_ + 1) * 16], der[:, s_ * 16:(s_ + 1) * 16], ALU.mult),
                                         reads=[("der", "ln"), ("der", "s1p", s_)], writes=[("der2", sp_)])
                                    S.op("dve", TT(b2, b2, modfm[:, (3 * s_) * 16:(3 * s_ + 1) * 16], ALU.add),
                                         reads=[("der2", sp_), ("modfm", 3 * s_)], writes=[("der2", sp_)])
                            elif v % 3 == 2:
                                S.op("dve", TS(der[:, 48 + s_ * 16:48 + (s_ + 1) * 16], modfm[:, vs], 1.0, wsub[s_], ALU.add, ALU.mult),
                                     reads=[("modfm", v)], writes=[("der", "gt", s_)])

                    pend[0] = fin
                yield
            if pend[0] is not None:
                pend[0]()
                pend[0] = None
            yield

        def mod_chunks(nlist, bufs, rowbanks, tbank):
            for _ in mod_gen(nlist, bufs, rowbanks, tbank):
                pass

        A.mark()
        mod_chunks(range(0, 8), mod_alloc(A), (0, 1), 2)
        A.release()

        out_ops = {}

        def layer_norm(s, lo, n, T, p, final_cols=None):
            xb, sq, t1, rstd, nmr, tt = T
            for c in range(KC):
                b = c % 2
                S.op("dve", CP(xb[b][:, :n], xs[:, c, lo:lo + n]), reads=[("x", c, lo)], writes=[("xb", b)])
                S.op("act", ACT(sq[b][:, :n], xs[:, c, lo:lo + n], AF.Square), reads=[("x", c, lo)], writes=[("sq", b)])
                S.op("pe", MM(banks[6][:, :n], cm[:, ONESD, :], xb[b][:, :n], c == 0, c == KC - 1),
                     reads=[("xb", b), "cm"], writes=[PB(6)])
                S.op("pe", MM(banks[7][:, :n], cm[:, ONESD, :], sq[b][:, :n], c == 0, c == KC - 1),
                     reads=[("sq", b), "cm"], writes=[PB(7)])
            S.op("act", ACT(t1[:, :n], banks[6][:, :n], AF.Square), reads=[PB(6)], writes=["t1"])
            S.op("dve", TT(t1[:, :n], banks[7][:, :n], t1[:, :n], ALU.subtract), reads=[PB(7), "t1"], writes=["t1"])
            S.op("act", ACT(t1[:, :n], t1[:, :n], AF.Sqrt, bias=epst[:, 0:1]), reads=["t1", "epst"], writes=["t1"])
            S.op("dve", RCP(rstd[:, :n], t1[:, :n]), reads=["t1"], writes=["rstd"])
            S.op("dve", STT(nmr[:, :n], banks[6][:, :n], -1.0, rstd[:, :n], ALU.mult, ALU.mult),
                 reads=[PB(6), "rstd"], writes=["nmr"])
            deferred = []
            for c in range(KC):
                b = c % 2
                S.op("dve", TT(tt[b][:, :n], xs[:, c, lo:lo + n], rstd[:, :n], ALU.mult),
                     reads=[("x", c, lo), "rstd"], writes=[("tt", b)])
                if s < 2:
                    S.op("dve", TT(xs[:, c, lo:lo + n], tt[b][:, :n], nmr[:, :n], ALU.add), reads=[("tt", b), "nmr"], writes=[("x", c, lo)])
                    S.op("act", ACT(us[:, c, lo:lo + n], xs[:, c, lo:lo + n], AF.Identity,
                                    bias=der2[:, 32 + s * 16 + c:32 + s * 16 + c + 1], scale=der2[:, s * 16 + c:s * 16 + c + 1]),
                         reads=[("x", c, lo), ("der2", s)], writes=[("u", c, lo)])
                    deferred.append(c)
                else:
                    S.op("dve", TT(tt[b][:, :n], tt[b][:, :n], nmr[:, :n], ALU.add), reads=[("tt", b), "nmr"], writes=[("tt", b)])
                    S.op("act", ACT(xs[:, c, lo:lo + n], tt[b][:, :n], AF.Identity, bias=lba(s, c), scale=lga(s, c)),
                         reads=[("tt", b), ("der", "ln")], writes=[("x", c, lo)])
                    if c % 4 == 3:
                        col = p * OWN + lo - 128
                        c4 = c // 4
                        out_ops[("out", c4)] = S.op(
                            "sp", DMA(outT[c4 * 512:(c4 + 1) * 512, col:col + n].rearrange("(c p) t -> p c t", p=128),
                                      xs[:, c4 * 4:c4 * 4 + 4, lo:lo + n]),
                            reads=[("x", cc_, lo) for cc_ in range(c4 * 4, c4 * 4 + 4)], dma_key=("out", c4))
            for c in deferred:
                S.op("act", ACT(xs[:, c, lo:lo + n], xs[:, c, lo:lo + n], AF.Identity, bias=lba(s, c), scale=lga(s, c)),
                     reads=[("x", c, lo), ("der", "ln")], writes=[("x", c, lo)])
                if DEBUG and lo >= 128 and c % 4 == 3:
                    dd = dbg1 if s == 0 else dbg2
                    col = p * OWN + lo - 128
                    c4 = c // 4
                    out_ops[("dbg", s, c4)] = S.op(
                        "sp", DMA(dd[c4 * 512:(c4 + 1) * 512, col:col + n].rearrange("(c p) t -> p c t", p=128),
                                  xs[:, c4 * 4:c4 * 4 + 4, lo:lo + n]),
                        reads=[("x", cc_, lo) for cc_ in range(c4 * 4, c4 * 4 + 4)], dma_key=("dbg", s, c4))

        def ln_temps(A):
            xb = [A.bf16("xb", 512) for _ in range(2)]
            sq = [A.bf16("sq", 512) for _ in range(2)]
            t1 = A.f32("t1", 512)
            rstd = A.f32("rstd", 512)
            nmr = A.f32("nmr", 512)
            tt = [A.f32("tt", 512) for _ in range(2)]
            return (xb, sq, t1, rstd, nmr, tt)

        def ffn(s, fi, tcs, p, inter=None, after_ln=None):
            A.mark()
            slt = [A.f32("slt", 512) for _ in range(2)]
            sub = A.sub(3584)
            wd = [A.bf16("wd", D) for _ in range(8)]
            wgu = [A.bf16("wgu", 16 * 256).rearrange("p (k j) -> p k j", k=16) for _ in range(2)]
            hb = [A.bf16("h", 4 * TP).rearrange("p (f t) -> p f t", f=4) for _ in range(2)]
            acc = [A.f32("acc", 512) for _ in range(2)]
            groups = [list(range(g * 4, min(NFC, g * 4 + 4))) for g in range(11)]
            cnt = [0, 0]
            LT = [None]

            def up(g):
                fcs = groups[g]
                for i0 in range(0, len(fcs), 2):
                    pair = fcs[i0:i0 + 2]
                    for fc in pair:
                        sl = fc % 2
                        S.op("pool", DMA(wgu[sl].rearrange("p k j -> p (k j)"), wgu_d[fi][fc]), writes=[("wgu", sl)],
                             dma_key=("wgu", sl))
                    for (lo, n) in tcs:
                        for fc in pair:
                            sl = fc % 2
                            f_i = fc - fcs[0]
                            b = cnt[0] % 2
                            cnt[0] += 1
                            pa, pb = 2 * b, 2 * b + 1
                            for kc in range(KC):
                                S.op("pe", MM(banks[pa][:, :n], wgu[sl][:, kc, 0:128], us[:, kc, lo:lo + n], kc == 0, kc == KC - 1),
                                     reads=[("wgu", sl), ("u", kc, lo)], writes=[PB(pa)])
                            for kc in range(KC):
                                S.op("pe", MM(banks[pb][:, :n], wgu[sl][:, kc, 128:256], us[:, kc, lo:lo + n], kc == 0, kc == KC - 1),
                                     reads=[("wgu", sl), ("u", kc, lo)], writes=[PB(pb)])
                            S.op("act", ACT(slt[b][:, :n], banks[pa][:, :n], AF.Silu), reads=[PB(pa)], writes=[("slt", b)])
                            S.op("dve", TT(hb[g % 2][:, f_i, lo:lo + n], slt[b][:, :n], banks[pb][:, :n], ALU.mult),
                                 reads=[("slt", b), PB(pb)], writes=[("h", g % 2, f_i, lo)])
                            if mgen[0] is not None:
                                if next(mgen[0], "done") == "done":
                                    mgen[0] = None
                for fc in fcs:
                    S.op("pool", DMA(wd[fc % 8], wd_d[fi][fc]), writes=[("wd", fc % 8)], dma_key=("wd", fc % 8))

            def down(g, last):
                fcs = groups[g]
                order = [(dc, t) for t in tcs for dc in range(KC)] if last else [(dc, t) for dc in range(KC) for t in tcs]
                for dc, (lo, n) in order:
                    if last:
                        bk = 4 + cnt[1] % 2
                    elif inter is not None:
                        bk = (4, 5, 7)[cnt[1] % 3]
                    else:
                        bk = 4 + cnt[1] % 4
                    cnt[1] += 1
                    for f_i, fc in enumerate(fcs):
                        S.op("pe", MM(banks[bk][:, :n], wd[fc % 8][:, dc * 128:(dc + 1) * 128], hb[g % 2][:, f_i, lo:lo + n],
                                      f_i == 0, f_i == len(fcs) - 1),
                             reads=[("wd", fc % 8), ("h", g % 2, f_i, lo)], writes=[PB(bk)])
                    if POOL_ACC and (not last) and cnt[1] % 3 == 2:
                        ab = (cnt[1] // 3) % 2
                        S.op("act", ACT(acc[ab][:, :n], banks[bk][:, :n], AF.Identity, scale=gt(s, dc)),
                             reads=[PB(bk), ("der", "gt", s)], writes=[("acc", ab)])
                        S.op("pool", lambda e, o_=xs[:, dc, lo:lo + n], a_=acc[ab][:, :n]: e.tensor_tensor(o_, o_, a_, ALU.add),
                             reads=[("acc", ab), ("x", dc, lo)], writes=[("x", dc, lo)])
                    else:
                        S.op("dve", STT(xs[:, dc, lo:lo + n], banks[bk][:, :n], gt(s, dc), xs[:, dc, lo:lo + n], ALU.mult, ALU.add),
                             reads=[PB(bk), ("x", dc, lo), ("der", "gt", s)], writes=[("x", dc, lo)])
                    if last and dc == KC - 1:
                        layer_norm(s, lo, n, LT[0], p)
                        if after_ln is not None:
                            after_ln(lo, n)

            mgen = [None]
            if inter is not None:
                mgen[0] = mod_gen(inter, mod_alloc(sub), (6, 6), 7)
            up(0)
            for g in range(1, 11):
                up(g)
                if g == 10:
                    assert mgen[0] is None or next(mgen[0], "done") == "done", "mod interleave not finished"
                    sub.reset()
                    LT[0] = ln_temps(sub)
                down(g - 1, False)
            down(10, True)
            sub.reset()
            A.release()

        def mixing(p, tcs_all, own):
            pass
            A.mark()
            base0 = A.top
            oT = A.top_bf16("oT", 8 * OWN).rearrange("p (c t) -> p c t", c=8)
            mixed = A.top_bf16("mixed", 8 * OWN).rearrange("p (c t) -> p c t", c=8)
            A.mark()
            r1 = [A.f32("r1", 512) for _ in range(2)]
            qz = [A.bf16("qz", 2 * OWN).rearrange("p (h t) -> p h t", h=2) for _ in range(2)]
            PT = [A.bf16("PT", 512).rearrange("p (k h q) -> p k h q", k=2, h=2) for _ in range(2)]
            dn = [A.f32("dn", 256) for _ in range(2)]
            A.pad_to(base0 + 4608)
            r2 = [A.f32("r2", 512) for _ in range(2)]
            qt = [A.bf16("qt", 512) for _ in range(2)]
            kT = A.bf16("kT", 2 * TP).rearrange("p (c t) -> p c t", c=2)
            V = A.bf16("V", 9 * 256).rearrange("p (b j) -> p b j", b=9)
            win = [A.bf16("win", 16 * 128).rearrange("p (k j) -> p k j", k=16) for _ in range(2)]
            wv = A.bf16("wv", 16 * 256).rearrange("p (k j) -> p k j", k=16)
            rC = A.f32("rC", TP)
            rS = A.f32("rS", TP)
            c0 = 0 if p == 0 else 1152
            s0 = 0 if p == 0 else 128
            nload = TP - s0
            wcnt = [0]
            pcnt = [0]

            def load_win(idx):
                sl = wcnt[0] % 2
                wcnt[0] += 1
                S.op("pool", DMA(win[sl].rearrange("p k j -> p (k j)"), win_d[idx]), writes=[("win", sl)], dma_key=("win", sl))
                return sl

            ksl = [load_win(8), load_win(9)]
            S.op("pool", DMA(wv.rearrange("p k j -> p (k j)"), wv_d), writes=["wv"], dma_key="wv")
            S.op("sp", DMA(rC[:, s0:TP], ropeC_d[:, c0:c0 + nload]), writes=["rC"], dma_key="rC")
            S.op("sp", DMA(rS[:, s0:TP], ropeS_d[:, c0:c0 + nload]), writes=["rS"], dma_key="rS")
            if p == 1:
                S.op("dve", CP(kT[:, :, 0:128], kcarry), reads=["kcarry"], writes=[("kT", 0, 0), ("kT", 1, 0)])
                S.op("dve", CP(V[:, 0, :], vcarry), reads=["vcarry"], writes=[("V", 0)])

            def proj_rope(sl, bcol, lo, n):
                b = pcnt[0] % 2
                pcnt[0] += 1
                for kc in range(KC):
                    S.op("pe", MM(banks[b][:, :n], win[sl][:, kc, :], us[:, kc, lo:lo + n], kc == 0, kc == KC - 1),
                         reads=[("win", sl), ("u", kc, lo)], writes=[PB(b)])
                S.op("act", ACT(qt[b][:, :n], banks[b][:, :n], AF.Identity, bias=biasfm[:, bcol:bcol + 1]),
                     reads=[PB(b), "biasfm"], writes=[("qt", b)])
                S.op("pe", MM(banks[2 + b][:, :n], cm[:, PERM, :], qt[b][:, :n], True, True), reads=[("qt", b), "cm"], writes=[PB(2 + b)])
                S.op("dve", TT(r1[b][:, :n], qt[b][:, :n], rC[:, lo:lo + n], ALU.mult), reads=[("qt", b), "rC"], writes=[("r1", b)])
                S.op("dve", TT(r2[b][:, :n], banks[2 + b][:, :n], rS[:, lo:lo + n], ALU.mult), reads=[PB(2 + b), "rS"], writes=[("r2", b)])
                return b

            for (lo, n) in (tcs_all if p == 0 else own):
                for j in range(2):
                    b = proj_rope(ksl[j], 16 + j, lo, n)
                    tags = [("kT", j, blk) for blk in range(lo // 128, (lo + n) // 128)]
                    S.op("dve", TT(kT[:, j, lo:lo + n], r1[b][:, :n], r2[b][:, :n], ALU.add), reads=[("r1", b), ("r2", b)], writes=tags)
                for blk in range(lo // 128, (lo + n) // 128):
                    b = pcnt[0] % 2
                    pcnt[0] += 1
                    for kc in range(KC):
                        S.op("pe", MM(banks[b][:, 0:256], us[:, kc, blk * 128:(blk + 1) * 128], wv[:, kc, :], kc == 0, kc == KC - 1),
                             reads=["wv", ("u", kc, lo)], writes=[PB(b)])
                    S.op("dve", TT(V[:, blk, :], banks[b][:, 0:256], bvbc, ALU.add), reads=[PB(b), "bvbc"], writes=[("V", blk)])
            if p == 0:
                S.op("dve", CP(kcarry, kT[:, :, 1024:1152]), reads=[("kT", 0, 8), ("kT", 1, 8)], writes=["kcarry"])
                S.op("dve", CP(vcarry, V[:, 8, :]), reads=[("V", 8)], writes=["vcarry"])
            for b in range(2):
                S.op("dve", MEMSET(qz[b][64:128, 0, :], 0.0), writes=[("qz", b)])
                S.op("dve", MEMSET(qz[b][0:64, 1, :], 0.0), writes=[("qz", b)])
            def qproj_gen(i):
                sl = load_win(i)
                qb = i % 2
                for (lo, n) in own:
                    b = pcnt[0] % 2
                    pcnt[0] += 1
                    for kc in range(KC):
                        S.op("pe", MM(banks[b][:, :n], win[sl][:, kc, :], us[:, kc, lo:lo + n], kc == 0, kc == KC - 1),
                             reads=[("win", sl), ("u", kc, lo)], writes=[PB(b)])
                        if kc % 8 == 7:
                            yield
                    S.op("act", ACT(qt[b][:, :n], banks[b][:, :n], AF.Identity, bias=biasfm[:, 8 + i:9 + i]),
                         reads=[PB(b), "biasfm"], writes=[("qt", b)])
                    S.op("pe", MM(banks[2 + b][:, :n], cm[:, PERM, :], qt[b][:, :n], True, True), reads=[("qt", b), "cm"], writes=[PB(2 + b)])
                    S.op("dve", TT(r1[b][:, :n], qt[b][:, :n], rC[:, lo:lo + n], ALU.mult), reads=[("qt", b), "rC"], writes=[("r1", b)])
                    S.op("dve", TT(r2[b][:, :n], banks[2 + b][:, :n], rS[:, lo:lo + n], ALU.mult), reads=[PB(2 + b), "rS"], writes=[("r2", b)])
                    o0 = lo - 128
                    S.op("dve", TT(qz[qb][0:64, 0, o0:o0 + n], r1[b][0:64, :n], r2[b][0:64, :n], ALU.add),
                         reads=[("r1", b), ("r2", b)], writes=[("qz", qb)])
                    S.op("dve", TT(qz[qb][64:128, 1, o0:o0 + n], r1[b][64:128, :n], r2[b][64:128, :n], ALU.add),
                         reads=[("r1", b), ("r2", b)], writes=[("qz", qb)])
                    yield

            for _ in qproj_gen(0):
                pass
            acnt = [0]
            for i in range(8):
                j = i // 4
                t_ = i % 4
                ha, hb_ = 8 * j + t_, 8 * j + 4 + t_
                qb = i % 2

                def scores(nb, it):
                    sb = 4 + it % 2
                    for kb_i, kblk in enumerate((nb - 1, nb)):
                        for h_i in range(2):
                            c_lo = kb_i * 256 + h_i * 128
                            if kb_i == 1:
                                mk = MASKD
                            else:
                                mk = MASKP0 if (p == 0 and nb == 1) else MASKP
                            S.op("pe", MM(banks[sb][:, c_lo:c_lo + 128], cm[:, IDENT, :], cm[:, mk, :], True, False),
                                 reads=["cm"], writes=[PB(sb)])
                            S.op("pe", MM(banks[sb][:, c_lo:c_lo + 128], kT[:, j, kblk * 128:(kblk + 1) * 128],
                                          qz[qb][:, h_i, (nb - 1) * 128:nb * 128], False, True),
                                 reads=[("kT", j, kblk), ("qz", qb)], writes=[PB(sb)])
                    pt = PT[it % 2]
                    bview = banks[sb][:, :].rearrange("p (k h q) -> p k h q", k=2, h=2)
                    for h_i, hd in enumerate((ha, hb_)):
                        S.op("act", ACT(pt[:, :, h_i, :], bview[:, :, h_i, :], AF.Exp, bias=nsink[:, hd:hd + 1], scale=0.125),
                             reads=[PB(sb), "nsink"], writes=[("PT", it % 2)])

                def pv(nb, it):
                    bk = 6 + it % 2
                    pt = PT[it % 2]
                    for kb_i, kblk in enumerate((nb - 1, nb)):
                        S.op("pe", MM(banks[bk][:, 0:256], V[:, kblk, j * 128:(j + 1) * 128],
                                      pt[:, kb_i, :, :].rearrange("p h q -> p (h q)"), kb_i == 0, kb_i == 1),
                             reads=[("V", kblk), ("PT", it % 2)], writes=[PB(bk)])
                    for kb_i, kblk in enumerate((nb - 1, nb)):
                        S.op("pe", MM(banks[bk][:, 256:512], cm[:, ONES, :],
                                      pt[:, kb_i, :, :].rearrange("p h q -> p (h q)"), kb_i == 0, kb_i == 1),
                             reads=["cm", ("PT", it % 2)], writes=[PB(bk)])
                    d_ = dn[it % 2]
                    S.op("act", ACT(d_, banks[bk][:, 256:512], AF.Identity, bias=onet[:, 0:1]), reads=[PB(bk), "onet"], writes=[("dn", it % 2)])
                    S.op("dve", RCP(d_, d_), reads=[("dn", it % 2)], writes=[("dn", it % 2)])
                    q0 = (nb - 1) * 128
                    S.op("dve", TT(oT[0:64, i, q0:q0 + 128], banks[bk][0:64, 0:128], d_[0:64, 0:128], ALU.mult),
                         reads=[PB(bk), ("dn", it % 2)], writes=[("oT", i)])
                    S.op("dve", TT(oT[64:128, i, q0:q0 + 128], banks[bk][64:128, 128:256], d_[64:128, 128:256], ALU.mult),
                         reads=[PB(bk), ("dn", it % 2)], writes=[("oT", i)])

                its = []
                for nb in range(1, 9):
                    its.append((nb, acnt[0]))
                    acnt[0] += 1
                gnext = qproj_gen(i + 1) if i + 1 < 8 else None
                for idx, (nb, it) in enumerate(its):
                    scores(nb, it)
                    if idx >= 1:
                        pv(*its[idx - 1])
                    if gnext is not None:
                        next(gnext, None)
                pv(*its[-1])
                if gnext is not None:
                    for _ in gnext:
                        pass
            A.release()
            pass
            A.mark()
            pooled = [A.bf16("pooled", 2 * OWN).rearrange("p (c t) -> p c t", c=2) for _ in range(2)]
            ptm = [A.f32("ptm", 528) for _ in range(2)]
            stm = [A.f32("stm", 528) for _ in range(2)]
            st2 = [A.f32("st2", 528) for _ in range(2)]
            t16 = A.f32("t16", 16)
            A.pad_to(base0 + 8192)
            win = [A.bf16("win", 16 * 128).rearrange("p (k j) -> p k j", k=16) for _ in range(2)]
            wpl = A.bf16("wpl", 4 * 2 * 256).rearrange("p (g c d) -> p g c d", g=4, c=2)
            S.op("pool", DMA(wpl.rearrange("p g c d -> p (g c d)"), wpool_d), writes=["wpl"], dma_key="wpl")
            pcnt = [0]
            wcnt = [0]
            for pc in range(8):
                g = pc // 2
                cc = pc % 2
                w_ = 2 ** (g + 1)
                sl = wcnt[0] % 2
                wcnt[0] += 1
                S.op("pool", DMA(win[sl].rearrange("p k j -> p (k j)"), win_d[10 + pc]), writes=[("win", sl)], dma_key=("win", sl))
                pbuf = pooled[g % 2]
                for (lo, n) in (tcs_all if p == 0 else own):
                    b = pcnt[0] % 2
                    pcnt[0] += 1
                    for kc in range(KC):
                        S.op("pe", MM(banks[b][:, :n], win[sl][:, kc, :], us[:, kc, lo:lo + n], kc == 0, kc == KC - 1),
                             reads=[("win", sl), ("u", kc, lo)], writes=[PB(b)])
                    if lo == 0:
                        S.op("dve", TS(pcarry[:, pc, :], banks[b][:, 112:128], biasfm[:, pc:pc + 1], hv[:, 0:1], ALU.add, ALU.mult),
                             reads=[PB(b), "biasfm", "hv"], writes=[("pcarry", pc)])
                        continue
                    pt_ = ptm[b]
                    S.op("act", ACT(pt_[:, 16:16 + n], banks[b][:, :n], AF.Identity, bias=biasfm[:, pc:pc + 1]),
                         reads=[PB(b), "biasfm"], writes=[("ptm", b)])
                    S.op("dve", CP(pt_[:, 0:16], pcarry[:, pc, :]), reads=[("pcarry", pc)], writes=[("ptm", b)])
                    S.op("dve", CP(pcarry[:, pc, :], pt_[:, n:n + 16]), reads=[("ptm", b)], writes=[("pcarry", pc)])
                    cur = pt_
                    curtag = ("ptm", b)
                    tmps = [(stm[b], ("stm", b)), (st2[b], ("st2", b))]
                    for k in range(g + 1):
                        sh = 1 << k
                        lo_k = 2 * sh - 1
                        dst, dtag = tmps[k % 2]
                        S.op("dve", TT(dst[:, lo_k:16 + n], cur[:, lo_k:16 + n], cur[:, lo_k - sh:16 + n - sh], ALU.add),
                             reads=[curtag], writes=[dtag])
                        cur, curtag = dst, dtag
                    o0 = lo - 128
                    S.op("dve", STT(pbuf[:, cc, o0:o0 + n], cur[:, 16:16 + n], 1.0 / w_, pt_[:, 16:16 + n], ALU.mult, ALU.subtract),
                         reads=[curtag, ("ptm", b)], writes=[("pooled", g % 2, cc)])
                    if p == 0 and lo == 128:
                        S.op("dve", TT(t16, cur[:, 16:32], invc[:, g * 16:(g + 1) * 16], ALU.mult), reads=[curtag, "invc"], writes=["t16"])
                        S.op("dve", TT(pbuf[:, cc, 0:16], t16, pt_[:, 16:32], ALU.subtract), reads=["t16", ("ptm", b)],
                             writes=[("pooled", g % 2, cc)])
                if cc == 1:
                    for dcc in range(2):
                        for (lo, n) in own:
                            b = 2 + pcnt[0] % 2
                            pcnt[0] += 1
                            o0 = lo - 128
                            for c2 in range(2):
                                S.op("pe", MM(banks[b][:, :n], wpl[:, g, c2, dcc * 128:(dcc + 1) * 128], pbuf[:, c2, o0:o0 + n], c2 == 0, c2 == 1),
                                     reads=["wpl", ("pooled", g % 2, c2)], writes=[PB(b)])
                            mc = 2 * g + dcc
                            S.op("act", ACT(mixed[:, mc, o0:o0 + n], banks[b][:, :n], AF.Identity, scale=biasfm[:, 52 + mc:53 + mc]),
                                 reads=[PB(b), "biasfm"], writes=[("mixed", mc)])
            A.release()
            for (lo, n) in own:
                o0 = lo - 128
                pass
                A.mark()
                merged = A.bf16("merged", 16 * 512).rearrange("p (c t) -> p c t", c=16)
                A.mark()
                ring = [A.bf16("ring", 16 * 128).rearrange("p (k j) -> p k j", k=16) for _ in range(6)]
                sab = [[A.f32("sa", 512), A.f32("sb", 512)] for _ in range(2)]
                for dc in range(KC):
                    par = dc % 2
                    rs = [ring[par * 3 + r] for r in range(3)]
                    for r in range(3):
                        S.op("pool", DMA(rs[r].rearrange("p k j -> p (k j)"), wmix_d[dc, r]), writes=[("ring", par * 3 + r)],
                             dma_key=("ring", par * 3 + r))
                    b0 = par * 4
                    for kc in range(8):
                        S.op("pe", MM(banks[b0][:, :n], rs[0][:, kc, :], mixed[:, kc, o0:o0 + n], kc == 0, kc == 7),
                             reads=[("ring", par * 3), ("mixed", kc)], writes=[PB(b0)])
                    for kc in range(8):
                        S.op("pe", MM(banks[b0 + 1][:, :n], rs[0][:, 8 + kc, :], oT[:, kc, o0:o0 + n], kc == 0, kc == 7),
                             reads=[("ring", par * 3), ("oT", kc)], writes=[PB(b0 + 1)])
                    for r in (1, 2):
                        for kc in range(KC):
                            S.op("pe", MM(banks[b0 + 1 + r][:, :n], rs[r][:, kc, :], us[:, kc, lo:lo + n], kc == 0, kc == KC - 1),
                                 reads=[("ring", par * 3 + r), ("u", kc, lo)], writes=[PB(b0 + 1 + r)])
                    sa, sb_ = sab[par]
                    S.op("act", ACT(sa[:, :n], banks[b0 + 2][:, :n], AF.Sigmoid, bias=biasfm[:, 20 + dc:21 + dc]),
                         reads=[PB(b0 + 2), "biasfm"], writes=[("sa", par)])
                    S.op("dve", TT(sa[:, :n], sa[:, :n], banks[b0][:, :n], ALU.mult), reads=[("sa", par), PB(b0)], writes=[("sa", par)])
                    S.op("act", ACT(sb_[:, :n], banks[b0 + 3][:, :n], AF.Sigmoid, bias=biasfm[:, 36 + dc:37 + dc]),
                         reads=[PB(b0 + 3), "biasfm"], writes=[("sb", par)])
                    S.op("dve", TT(sb_[:, :n], sb_[:, :n], banks[b0 + 1][:, :n], ALU.mult), reads=[("sb", par), PB(b0 + 1)], writes=[("sb", par)])
                    S.op("dve", TT(merged[:, dc, :n], sa[:, :n], sb_[:, :n], ALU.add), reads=[("sa", par), ("sb", par)], writes=[("merged", dc)])
                A.release()
                pass
                A.mark()
                wo = [A.bf16("wo", 16 * 128).rearrange("p (k j) -> p k j", k=16) for _ in range(4)]
                if (lo, n) != own[-1]:
                    A.pad_to(base0 + 10240)
                LT = ln_temps(A)
                for d2 in range(KC):
                    sl = d2 % 4
                    S.op("pool", DMA(wo[sl].rearrange("p k j -> p (k j)"), wo_d[d2]), writes=[("wo", sl)], dma_key=("wo", sl))
                    bk = d2 % 4
                    for kc in range(KC):
                        S.op("pe", MM(banks[bk][:, :n], wo[sl][:, kc, :], merged[:, kc, :n], kc == 0, kc == KC - 1),
                             reads=[("wo", sl), ("merged", kc)], writes=[PB(bk)])
                    S.op("dve", STT(xs[:, d2, lo:lo + n], banks[bk][:, :n], gt(1, d2), xs[:, d2, lo:lo + n], ALU.mult, ALU.add),
                         reads=[PB(bk), ("x", d2, lo), ("der", "gt", 1)], writes=[("x", d2, lo)])
                layer_norm(1, lo, n, LT, p)
                A.release()
                A.release()
            A.release_top()
            A.release()

        epst = A.f32("epst", 1)
        onet = A.f32("onet", 1)
        S.op("dve", MEMSET(epst, LN_EPS), writes=["epst"])
        S.op("dve", MEMSET(onet, 1.0), writes=["onet"])

        inter0 = list(range(8, 36))
        own = [(128, 512), (640, 512)]

        def prologue(p, lo, n):
            col = lo if p == 0 else 1024 + lo
            for c4 in range(4):
                S.op("sp", DMA(xs[:, c4 * 4:c4 * 4 + 4, lo:lo + n],
                               xT[c4 * 512:(c4 + 1) * 512, col:col + n].rearrange("(c p) t -> p c t", p=128)),
                     writes=[("x", c, lo) for c in range(c4 * 4, c4 * 4 + 4)], dma_key=("x", c4, lo))
            for c in range(KC):
                S.op("dve", TS(us[:, c, lo:lo + n], xs[:, c, lo:lo + n], s1p(0, c), shift(0, c), ALU.mult, ALU.add),
                     reads=[("x", c, lo), ("der", "s1p", 0), ("modfm", 0)], writes=[("u", c, lo)])
                S.op("act", ACT(xs[:, c, lo:lo + n], xs[:, c, lo:lo + n], AF.Identity, scale=ALPHA),
                     reads=[("x", c, lo)], writes=[("x", c, lo)])

        for (lo, n) in [(0, 128)] + own:
            prologue(0, lo, n)
        ffn(0, 0, [(0, 128)] + own, 0, inter=inter0)
        mixing(0, [(0, 128)] + own, own)
        ffn(2, 1, own, 0, after_ln=lambda lo, n: prologue(1, lo, n))
        ffn(0, 0, own, 1)
        mixing(1, own, own)
        ffn(2, 1, own, 1)

        S.emit(final_waits=list(out_ops.values()))
    return nc


def _shared_layout(inp):
    f = np.float32
    sh = {}
    w_ada = inp["w_ada"][0]
    sh["wada"] = np.ascontiguousarray(
        w_ada.reshape(4, 4, 128, 36, 512).transpose(3, 0, 2, 1, 4).reshape(144, 128, 4 * 512))
    sh["bada"] = np.ascontiguousarray(inp["b_ada"][0].reshape(9, 16, 128).transpose(2, 0, 1).reshape(128, 144))
    g = inp["ln_g"][0].reshape(3, 16, 128).transpose(2, 0, 1).reshape(128, 48)
    b = inp["ln_b"][0].reshape(3, 16, 128).transpose(2, 0, 1).reshape(128, 48)
    sh["lngb"] = np.ascontiguousarray(np.concatenate([g, b], axis=1))
    for nm, kin, kout in (("1", "w_ffn1_in", "w_ffn1_out"), ("2", "w_ffn2_in", "w_ffn2_out")):
        wi = inp[kin][0].reshape(16, 128, 2, NFC, 128)
        sh["wgu" + nm] = np.ascontiguousarray(wi.transpose(3, 1, 0, 2, 4).reshape(NFC, 128, 16 * 256))
        sh["wd" + nm] = np.ascontiguousarray(inp[kout][0].reshape(NFC, 128, D))
    w_in = inp["w_in"][0]
    b_in = inp["b_in"][0]
    qcols = []
    for i in range(8):
        j, t = i // 4, i % 4
        for h in (8 * j + t, 8 * j + 4 + t):
            qcols.extend(range(1024 + h * 64, 1024 + (h + 1) * 64))
    qcols = np.array(qcols)
    cols = np.concatenate([qcols, np.arange(2048, 2304), np.arange(0, 1024)])
    wsel = w_in[:, cols].reshape(16, 128, 18, 128)
    sh["win"] = np.ascontiguousarray(wsel.transpose(2, 1, 0, 3).reshape(18, 128, 16 * 128))
    sh["wv"] = np.ascontiguousarray(w_in[:, 2304:2560].reshape(16, 128, 256).transpose(1, 0, 2).reshape(128, 16 * 256))
    wa = inp["w_branch_a"][0].reshape(8, 128, 16, 128)
    wb_rows = qcols - 1024
    wb = inp["w_branch_b"][0][wb_rows].reshape(8, 128, 16, 128)
    wab = np.concatenate([wa, wb], axis=0).transpose(2, 1, 0, 3).reshape(16, 128, 16 * 128)
    wga = w_in[:, 2560:4608].reshape(16, 128, 16, 128).transpose(2, 1, 0, 3).reshape(16, 128, 16 * 128)
    wgb = w_in[:, 4608:6656].reshape(16, 128, 16, 128).transpose(2, 1, 0, 3).reshape(16, 128, 16 * 128)
    sh["wmix"] = np.ascontiguousarray(np.stack([wab, wga, wgb], axis=1))
    sh["wo"] = np.ascontiguousarray(inp["w_out"][0].reshape(16, 128, 16, 128).transpose(2, 1, 0, 3).reshape(16, 128, 16 * 128))
    sh["wpool"] = np.ascontiguousarray(inp["w_pool"][0].reshape(4, 2, 128, 256).transpose(2, 0, 1, 3).reshape(128, 4 * 2 * 256))
    bias = np.zeros((128, 64), f)
    bias[:, 0:8] = b_in[0:1024].reshape(8, 128).T
    bias[:, 8:16] = b_in[qcols].reshape(8, 128).T
    bias[:, 16:18] = b_in[2048:2304].reshape(2, 128).T
    bias[:, 20:36] = b_in[2560:4608].reshape(16, 128).T
    bias[:, 36:52] = b_in[4608:6656].reshape(16, 128).T
    bias[:, 52:60] = inp["pool_scale"][0].reshape(8, 128).T
    sh["biasfm"] = bias
    sh["bvbc"] = np.ascontiguousarray(np.broadcast_to(b_in[2304:2560][None, :], (128, 256))).astype(f)
    sh["sinkbc"] = np.ascontiguousarray(np.broadcast_to(inp["sinks"][0][None, :], (128, 16))).astype(f)
    return sh


def _core_consts(half):
    f = np.float32
    inv_freq = (500000.0 ** (-np.arange(0, 16, 2, dtype=f) / 16)).astype(f)
    pos = np.concatenate([np.arange(half * 2048 - 128, half * 2048), np.arange(half * 2048, half * 2048 + 2048)]).astype(f)
    pos = np.maximum(pos, 0).astype(f)
    ang = (pos[:, None] * inv_freq[None, :]).astype(f)
    cos, sin = np.cos(ang).astype(f), np.sin(ang).astype(f)
    rC = np.ones((128, 2176), f)
    rS = np.zeros((128, 2176), f)
    for hh in range(2):
        for d in range(16):
            pp = hh * 64 + d
            rC[pp] = cos[:, d % 8]
            rS[pp] = -sin[:, d % 8] if d < 8 else sin[:, d % 8]
    cm = np.zeros((7, 128, 128), f)
    cm[0] = np.eye(128, dtype=f)
    jj = np.arange(128)[:, None]
    ii = np.arange(128)[None, :]
    cm[1] = np.where(jj > ii, 0.0, NEG)
    cm[2] = np.where(jj <= ii, 0.0, NEG)
    cm[3] = cm[1] if half == 1 else np.full((128, 128), NEG, f)
    for m in range(128):
        d = m % 64
        if d < 16:
            k = m + 8 if d < 8 else m - 8
            cm[4][k, m] = 1.0
    cm[5] = 1.0 / 2048.0
    cm[6] = 1.0
    cmat = np.ascontiguousarray(cm.transpose(1, 0, 2).reshape(128, 7 * 128))
    invc = np.zeros((128, 64), f)
    for g in range(4):
        w = 2 ** (g + 1)
        t = np.arange(16)
        v = (1.0 / np.minimum(t + 1, w)) if half == 0 else np.full(16, 1.0 / w)
        invc[:, g * 16:(g + 1) * 16] = v.astype(f)[None, :]
    hv = np.full((128, 1), float(half), f)
    return {"ropeC": rC, "ropeS": rS, "cmat": cmat, "invcnt": invc, "hv": hv}


_CACHE = {}


def kernel(**inputs):
    inp = {k: np.asarray(v) for k, v in inputs.items()}
    if "nc" not in _CACHE:
        _CACHE["nc"] = build_program()
    nc = _CACHE["nc"]
    sh = _shared_layout(inp)
    x = inp["x"]
    c = inp["c"]
    in_maps = []
    for core in range(8):
        b, half = core // 2, core % 2
        m = dict(sh)
        xt = np.zeros((D, 2176), np.float32)
        if half == 1:
            xt[:, 0:128] = x[b, 2048 - 128:2048].T
        xt[:, 128:] = x[b, half * 2048:(half + 1) * 2048].T
        m["xT"] = xt
        m["cfm"] = np.ascontiguousarray(c[b].reshape(16, 128).T)
        m.update(_core_consts(half))
        in_maps.append(m)
    res = run_bass_kernel_spmd(nc, in_maps, core_ids=list(range(8)))
    _CACHE["res"] = res
    out = np.empty((4, 4096, D), np.float32)
    for core in range(8):
        b, half = core // 2, core % 2
        out[b, half * 2048:(half + 1) * 2048] = res.results[core]["outT"].T
    return out
```

```python
import contextlib
import numpy as np
import concourse.bass as bass
import concourse.mybir as mybir
from concourse.bass_utils import run_bass_kernel_spmd

F32 = mybir.dt.float32
BF16 = mybir.dt.bfloat16
AF = mybir.ActivationFunctionType
ALU = mybir.AluOpType

D = 2048
KC = 16
DFF = 5504
NFC = 43
TP = 1152
OWN = 1024
ALPHA = 2.0 ** 0.25
LN_EPS = 1e-5
NEG = -240000.0
NWORDS = 53100
DEBUG = False
POOL_ACC = False

ENGS = ("pe", "act", "dve", "pool", "sp")


class Op:
    __slots__ = ("eng", "fn", "deps", "dma_key", "dma_n", "signal", "sigval", "idx", "cdeps", "ddeps")

    def __init__(self, eng, fn, dma_key):
        self.eng = eng
        self.fn = fn
        self.deps = []
        self.dma_key = dma_key
        self.dma_n = 0
        self.signal = False
        self.sigval = 0


class Sched:
    def __init__(self, nc):
        self.nc = nc
        self.ops = {e: [] for e in ENGS}
        self.last_writer = {}
        self.readers = {}
        self.dma_count = {}
        self.pending = {e: [] for e in ENGS}
        self.fam_last = {}
        self.fam_pending = {}

    def barrier(self):
        last = {e: (self.ops[e][-1] if self.ops[e] else None) for e in ENGS}
        for e in ENGS:
            for e2 in ENGS:
                if e2 != e and last[e2] is not None:
                    self.pending[e].append(last[e2])

    def op(self, eng, fn, reads=(), writes=(), dma_key=None):
        o = Op(eng, fn, dma_key)
        is_dma = dma_key is not None
        deps = {}
        for t in reads:
            w = self.last_writer.get(t)
            if w is not None:
                deps[id(w)] = (w, "raw")
        for t in writes:
            w = self.last_writer.get(t)
            if w is not None and id(w) not in deps:
                deps[id(w)] = (w, "waw")
            for r in self.readers.get(t, ()):
                if id(r) not in deps:
                    deps[id(r)] = (r, "war")
        for p, kind in deps.values():
            p_dma = p.dma_key is not None
            if not p_dma and not is_dma and p.eng == eng:
                if eng == "pe" or kind != "raw":
                    continue
            o.deps.append(p)
        if self.pending[eng]:
            o.deps.extend(self.pending[eng])
            self.pending[eng] = []
        lkey = ("dma", dma_key) if is_dma else eng
        for t in tuple(reads) + tuple(writes):
            fam = t[0] if isinstance(t, tuple) else t
            fp = self.fam_pending.get(fam)
            if fp is not None and eng not in fp[1]:
                fp[1].add(eng)
                o.deps.extend(fp[0])
            self.fam_last.setdefault(fam, {})[lkey] = o
        for t in writes:
            self.last_writer[t] = o
            self.readers[t] = []
        for t in reads:
            self.readers.setdefault(t, []).append(o)
        if is_dma:
            n = self.dma_count.get(dma_key, 0) + 1
            self.dma_count[dma_key] = n
            o.dma_n = n
        o.idx = len(self.ops[eng])
        self.ops[eng].append(o)
        return o

    def emit(self, final_waits=()):
        nc = self.nc
        for e in ENGS:
            for o in self.ops[e]:
                cd, dd = {}, {}
                for p in o.deps:
                    if p.dma_key is None:
                        q = cd.get(p.eng)
                        if q is None or p.idx > q.idx:
                            cd[p.eng] = p
                    else:
                        q = dd.get(p.dma_key)
                        if q is None or p.dma_n > q.dma_n:
                            dd[p.dma_key] = p
                o.cdeps, o.ddeps = cd, dd
                for p in cd.values():
                    p.signal = True
        for o in final_waits:
            if o.dma_key is None:
                o.signal = True
        for e in ENGS:
            c = 0
            for o in self.ops[e]:
                if o.dma_key is None and o.signal:
                    c += 1
                    o.sigval = c
        with contextlib.ExitStack() as st:
            esem = {e: st.enter_context(nc.semaphore("s_" + e)) for e in ENGS}
            dsem = {}
            for i, k in enumerate(self.dma_count):
                dsem[k] = st.enter_context(nc.semaphore("d%d" % i))
            print("sched: ops", {e: len(self.ops[e]) for e in ENGS}, "dma keys", len(self.dma_count))
            block = st.enter_context(nc.Block())

            def run(e, eng):
                known = {}
                for o in self.ops[e]:
                    need = {}
                    for p in o.ddeps.values():
                        s = dsem[p.dma_key]
                        need[id(s)] = (s, 16 * p.dma_n)
                    for p in o.cdeps.values():
                        s = esem[p.eng]
                        need[id(s)] = (s, p.sigval)
                    for key, (s, v) in need.items():
                        if known.get(key, 0) >= v:
                            continue
                        eng.wait_ge(s, v)
                        known[key] = v
                    ins = o.fn(eng)
                    if o.dma_key is not None:
                        ins.then_inc(dsem[o.dma_key], 16)
                    elif o.signal:
                        ins.then_inc(esem[e], 1)
                if e == "sp":
                    for o in final_waits:
                        if o.dma_key is not None:
                            eng.wait_ge(dsem[o.dma_key], 16 * o.dma_n)
                        else:
                            eng.wait_ge(esem[o.eng], o.sigval)

            @block.tensor
            def _(eng):
                run("pe", eng)

            @block.scalar
            def _(eng):
                run("act", eng)

            @block.vector
            def _(eng):
                run("dve", eng)

            @block.gpsimd
            def _(eng):
                run("pool", eng)

            @block.sync
            def _(eng):
                run("sp", eng)


class Arena:
    def __init__(self, t, sched):
        self.t = t
        self.S = sched
        self.top = 0
        self.marks = []
        self.live = []
        self.dead = []
        self.hi = NWORDS
        self.live_top = []

    def _alloc(self, fam, words):
        w8 = (words + 7) // 8 * 8
        off = self.top
        self.top += w8
        assert self.top <= self.hi, ("arena overflow", fam, self.top, self.hi)
        ops = []
        for (f0, o0, s0, snap) in self.dead:
            if o0 < off + w8 and off < o0 + s0:
                ops.extend(snap)
        if ops:
            prev = self.S.fam_pending.get(fam)
            if prev is not None:
                ops = list(prev[0]) + ops
            uniq = {id(o): o for o in ops}
            self.S.fam_pending[fam] = (list(uniq.values()), set())
        self.live.append((fam, off, w8))
        return off

    def top_bf16(self, fam, n):
        w = (n + 1) // 2
        w8 = (w + 7) // 8 * 8
        self.hi -= w8
        off = self.hi
        assert self.top <= self.hi, ("arena overflow (top)", fam)
        ops = []
        for (f0, o0, s0, snap) in self.dead:
            if o0 < off + w8 and off < o0 + s0:
                ops.extend(snap)
        if ops:
            prev = self.S.fam_pending.get(fam)
            if prev is not None:
                ops = list(prev[0]) + ops
            uniq = {id(o): o for o in ops}
            self.S.fam_pending[fam] = (list(uniq.values()), set())
        self.live_top.append((fam, off, w8))
        return self.t[:, off:off + w].bitcast(BF16)

    def release_top(self):
        for (fam, off, sz) in self.live_top:
            snap = list(self.S.fam_last.get(fam, {}).values())
            self.dead.append((fam, off, sz, snap))
        self.live_top = []
        self.hi = NWORDS

    def f32(self, fam, n):
        off = self._alloc(fam, n)
        return self.t[:, off:off + n]

    def bf16(self, fam, n):
        w = (n + 1) // 2
        off = self._alloc(fam, w)
        return self.t[:, off:off + w].bitcast(BF16)

    def pad_to(self, off):
        assert off >= self.top, ("pad_to backwards", off, self.top)
        self.top = off

    def sub(self, words):
        w8 = (words + 7) // 8 * 8
        c = Arena(self.t, self.S)
        c.base = self.top
        c.top = self.top
        c.hi = self.top + w8
        c.dead = self.dead
        self.top += w8
        assert self.top <= self.hi
        return c

    def reset(self):
        for (fam, off, sz) in self.live:
            snap = list(self.S.fam_last.get(fam, {}).values())
            self.dead.append((fam, off, sz, snap))
        self.live = []
        self.top = self.base

    def mark(self):
        self.marks.append((self.top, len(self.live)))

    def release(self):
        top, nl = self.marks.pop()
        for (fam, off, sz) in self.live[nl:]:
            snap = list(self.S.fam_last.get(fam, {}).values())
            self.dead.append((fam, off, sz, snap))
        del self.live[nl:]
        self.top = top


def MM(out, lhsT, rhs, start, stop):
    return lambda e: e.matmul(out, lhsT, rhs, start=start, stop=stop)


def ACT(out, in_, func, bias=None, scale=None):
    kw = {}
    if bias is not None:
        kw["bias"] = bias
    if scale is not None:
        kw["scale"] = scale
    return lambda e: e.activation(out, in_, func, **kw)


def TT(out, a, b, op):
    return lambda e: e.tensor_tensor(out, a, b, op)


def TS(out, a, s1, s2, op0, op1):
    return lambda e: e.tensor_scalar(out, a, s1, s2, op0, op1)


def TS1(out, a, s1, op0):
    return lambda e: e.tensor_scalar(out, a, s1, None, op0)


def STT(out, a, s, b, op0, op1):
    return lambda e: e.scalar_tensor_tensor(out, a, s, b, op0, op1)


def CP(out, a):
    return lambda e: e.tensor_copy(out, a)


def RCP(out, a):
    return lambda e: e.reciprocal(out, a)


def DMA(out, in_):
    return lambda e: e.dma_start(out=out, in_=in_)


def MEMSET(out, v):
    return lambda e: e.memset(out, v)


def build_program():
    nc = bass.Bass("TRN2", target_bir_lowering=False)

    def din(name, shape):
        return nc.dram_tensor(name, list(shape), F32, kind="ExternalInput").ap()

    xT = din("xT", [D, 2176])
    cfm_d = din("cfm", [128, 16])
    wada_d = din("wada", [144, 128, 4 * 512])
    bada_d = din("bada", [128, 144])
    lngb_d = din("lngb", [128, 96])
    wgu_d = [din("wgu1", [NFC, 128, 16 * 256]), din("wgu2", [NFC, 128, 16 * 256])]
    wd_d = [din("wd1", [NFC, 128, D]), din("wd2", [NFC, 128, D])]
    win_d = din("win", [18, 128, 16 * 128])
    wv_d = din("wv", [128, 16 * 256])
    wmix_d = din("wmix", [16, 3, 128, 16 * 128])
    wo_d = din("wo", [16, 128, 16 * 128])
    wpool_d = din("wpool", [128, 4 * 2 * 256])
    bias_d = din("biasfm", [128, 64])
    bv_d = din("bvbc", [128, 256])
    sink_d = din("sinkbc", [128, 16])
    ropeC_d = din("ropeC", [128, 2176])
    ropeS_d = din("ropeS", [128, 2176])
    cmat_d = din("cmat", [128, 10 * 128])
    invc_d = din("invcnt", [128, 64])
    hv_d = din("hv", [128, 1])
    outT = nc.dram_tensor("outT", [D, 2048], F32, kind="ExternalOutput").ap()
    if DEBUG:
        dbg1 = nc.dram_tensor("dbg1", [D, 2048], F32, kind="ExternalOutput").ap()
        dbg2 = nc.dram_tensor("dbg2", [D, 2048], F32, kind="ExternalOutput").ap()

    S = Sched(nc)
    with contextlib.ExitStack() as st:
        arena_t = st.enter_context(nc.sbuf_tensor("arena", [128, NWORDS], F32))
        banks = [st.enter_context(nc.psum_tensor("bank%d" % i, [128, 512], F32)) for i in range(8)]
        A = Arena(arena_t, S)
        PB = lambda i: ("ps", i)

        xs = A.f32("x", KC * TP).rearrange("p (c t) -> p c t", c=KC)
        us = A.bf16("u", KC * TP).rearrange("p (c t) -> p c t", c=KC)
        cm = A.bf16("cm", 10 * 128).rearrange("p (m j) -> p m j", m=10)
        IDENT, MASKP, MASKD, MASKP0, PERM, ONESD, ONES = 0, 1, 3, 5, 7, 8, 9
        cfm = A.f32("cfm", 16)
        cact = A.bf16("cact", 16)
        modfm = A.f32("modfm", 144)
        badafm = A.f32("bada", 144)
        lngb = A.f32("lngb", 96)
        der = A.f32("der", 48 + 48 + 96)
        der2 = A.f32("der2", 64)
        modraw = A.f32("modraw", 144)
        biasfm = A.f32("biasfm", 64)
        bvbc = A.f32("bvbc", 256)
        nsink = A.f32("nsink", 16)
        invc = A.f32("invc", 64)
        hv = A.f32("hv", 1)
        onebf = A.bf16("onebf", 2)
        pcarry = A.f32("pcarry", 8 * 16).rearrange("p (c t) -> p c t", c=8)
        kcarry = A.bf16("kcarry", 2 * 128).rearrange("p (c t) -> p c t", c=2)
        vcarry = A.bf16("vcarry", 256)

        def s1p(s, c):
            return der[:, s * 16 + c:s * 16 + c + 1]

        def gt(s, c):
            return der[:, 48 + s * 16 + c:48 + s * 16 + c + 1]

        def lga(s, c):
            return der[:, 96 + s * 16 + c:96 + s * 16 + c + 1]

        def lba(s, c):
            return der[:, 144 + s * 16 + c:144 + s * 16 + c + 1]

        def shift(s, c):
            return modfm[:, (3 * s) * 16 + c:(3 * s) * 16 + c + 1]

        S.op("pool", DMA(cm.rearrange("p m j -> p (m j)"), cmat_d), writes=["cm"], dma_key="cm")
        for nm, dst, src in (("cfm", cfm, cfm_d), ("bada", badafm, bada_d), ("lngb", lngb, lngb_d),
                             ("biasfm", biasfm, bias_d), ("bvbc", bvbc, bv_d), ("nsink", nsink, sink_d),
                             ("invc", invc, invc_d), ("hv", hv, hv_d)):
            S.op("sp", DMA(dst, src), writes=[nm], dma_key=nm)
        S.op("dve", TS1(nsink, nsink, -1.0, ALU.mult), reads=["nsink"], writes=["nsink"])
        S.op("dve", MEMSET(onebf, 1.0), writes=["onebf"])
        S.op("act", ACT(cact, cfm, AF.Silu), reads=["cfm"], writes=["cact"])

        wsub = (0.5, 1.0, 0.5)
        for s_ in range(3):
            a_ = ALPHA if s_ < 2 else 1.0
            S.op("dve", TS1(der[:, 96 + s_ * 16:96 + (s_ + 1) * 16], lngb[:, s_ * 16:(s_ + 1) * 16], a_, ALU.mult),
                 reads=["lngb"], writes=[("der", "ln")])
            S.op("dve", TS1(der[:, 144 + s_ * 16:144 + (s_ + 1) * 16], lngb[:, 48 + s_ * 16:48 + (s_ + 1) * 16], a_, ALU.mult),
                 reads=["lngb"], writes=[("der", "ln")])

        def mod_alloc(ar):
            wada = [ar.bf16("wada", 4 * 512).rearrange("p (k j) -> p k j", k=4) for _ in range(2)]
            rowf = ar.f32("rowf", 512)
            rowh = ar.bf16("rowh", 512)
            rowl = ar.bf16("rowl", 512)
            return (wada, rowf, rowh, rowl)

        def mod_gen(nlist, bufs, rowbanks, tbank):
            wada, rowf, rowh, rowl = bufs
            tiles = [(n, q) for n in nlist for q in range(4)]

            def dma(i):
                n, q = tiles[i]
                t = n * 4 + q
                S.op("pool", DMA(wada[t % 2].rearrange("p k j -> p (k j)"), wada_d[t]), writes=[("wada", t % 2)],
                     dma_key=("wada", t % 2))

            dma(0)
            pend = [None]
            for i, (n, q) in enumerate(tiles):
                t = n * 4 + q
                sl = t % 2
                rb = rowbanks[n % 2]
                for k4 in range(4):
                    kc = q * 4 + k4
                    S.op("pe", MM(banks[rb][0:1, :], cact[:, kc:kc + 1], wada[sl][:, k4, :], kc == 0, kc == 15),
                         reads=[("wada", sl), "cact"], writes=[PB(rb)])
                if i + 1 < len(tiles):
                    dma(i + 1)
                if pend[0] is not None:
                    pend[0]()
                    pend[0] = None
                if q == 3:
                    S.op("act", ACT(rowf[0:1, :], banks[rb][0:1, :], AF.Identity), reads=[PB(rb)], writes=["rowf"])
                    S.op("dve", CP(rowh[0:1, :], rowf[0:1, :]), reads=["rowf"], writes=["rowh"])
                    S.op("dve", TT(rowl[0:1, :], rowf[0:1, :], rowh[0:1, :], ALU.subtract), reads=["rowf", "rowh"], writes=["rowl"])

                    def fin(n=n):
                        for j in range(4):
                            S.op("pe", MM(banks[tbank][:, j:j + 1], rowh[0:1, j * 128:(j + 1) * 128], onebf[0:1, 0:1], True, False),
                                 reads=["rowh", "onebf"], writes=[PB(tbank)])
                            S.op("pe", MM(banks[tbank][:, j:j + 1], rowl[0:1, j * 128:(j + 1) * 128], onebf[0:1, 0:1], False, True),
                                 reads=["rowl", "onebf"], writes=[PB(tbank)])
                        S.op("dve", CP(modraw[:, n * 4:n * 4 + 4], banks[tbank][:, 0:4]), reads=[PB(tbank)], writes=[("modraw", n)])
                        if n % 4 == 3:
                            v = n // 4
                            s_ = v // 3
                            vs = slice(v * 16, (v + 1) * 16)
                            S.op("dve", TT(modfm[:, vs], modraw[:, vs], badafm[:, vs], ALU.add),
                                 reads=[("modraw", n_) for n_ in range(n - 3, n + 1)] + ["bada"], writes=[("modfm", v)])
                            if v % 3 == 1:
                                S.op("dve", TS(der[:, s_ * 16:(s_ + 1) * 16], modfm[:, vs], 1.0, (1.0 if s_ == 0 else 1.0 / ALPHA), ALU.add, ALU.mult),
                                     reads=[("modfm", v)], writes=[("der", "s1p", s_)])
                                if s_ >= 1:
                                    sp_ = s_ - 1
                                    g2 = der2[:, sp_ * 16:(sp_ + 1) * 16]
                                    b2 = der2[:, 32 + sp_ * 16:32 + (sp_ + 1) * 16]
                                    S.op("dve", TT(g2, der[:, 96 + sp_ * 16:96 + (sp_ + 1) * 16], der[:, s_ * 16:(s_ + 1) * 16], ALU.mult),
                                         reads=[("der", "ln"), ("der", "s1p", s_)], writes=[("der2", sp_)])
                                    S.op("dve", TT(b2, der[:, 144 + sp_ * 16:144 + (sp_ + 1) * 16], der[:, s_ * 16:(s_ + 1) * 16], ALU.mult),
                                         reads=[("der", "ln"), ("der", "s1p", s_)], writes=[("der2", sp_)])
                                    S.op("dve", TT(b2, b2, modfm[:, (3 * s_) * 16:(3 * s_ + 1) * 16], ALU.add),
                                         reads=[("der2", sp_), ("modfm", 3 * s_)], writes=[("der2", sp_)])
                            elif v % 3 == 2:
                                S.op("dve", TS(der[:, 48 + s_ * 16:48 + (s_ + 1) * 16], modfm[:, vs], 1.0, wsub[s_], ALU.add, ALU.mult),
                                     reads=[("modfm", v)], writes=[("der", "gt", s_)])

                    pend[0] = fin
                yield
            if pend[0] is not None:
                pend[0]()
                pend[0] = None
            yield

        def mod_chunks(nlist, bufs, rowbanks, tbank):
            for _ in mod_gen(nlist, bufs, rowbanks, tbank):
                pass

        A.mark()
        mod_chunks(range(0, 8), mod_alloc(A), (0, 1), 2)
        A.release()

        out_ops = {}

        def layer_norm(s, lo, n, T, p, final_cols=None):
            xb, sq, t1, rstd, nmr, tt = T
            for c in range(KC):
                b = c % 2
                S.op("dve", CP(xb[b][:, :n], xs[:, c, lo:lo + n]), reads=[("x", c, lo)], writes=[("xb", b)])
                S.op("act", ACT(sq[b][:, :n], xs[:, c, lo:lo + n], AF.Square), reads=[("x", c, lo)], writes=[("sq", b)])
                S.op("pe", MM(banks[6][:, :n], cm[:, ONESD, :], xb[b][:, :n], c == 0, c == KC - 1),
                     reads=[("xb", b), "cm"], writes=[PB(6)])
                S.op("pe", MM(banks[7][:, :n], cm[:, ONESD, :], sq[b][:, :n], c == 0, c == KC - 1),
                     reads=[("sq", b), "cm"], writes=[PB(7)])
            S.op("act", ACT(t1[:, :n], banks[6][:, :n], AF.Square), reads=[PB(6)], writes=["t1"])
            S.op("dve", TT(t1[:, :n], banks[7][:, :n], t1[:, :n], ALU.subtract), reads=[PB(7), "t1"], writes=["t1"])
            S.op("act", ACT(t1[:, :n], t1[:, :n], AF.Sqrt, bias=epst[:, 0:1]), reads=["t1", "epst"], writes=["t1"])
            S.op("dve", RCP(rstd[:, :n], t1[:, :n]), reads=["t1"], writes=["rstd"])
            S.op("dve", STT(nmr[:, :n], banks[6][:, :n], -1.0, rstd[:, :n], ALU.mult, ALU.mult),
                 reads=[PB(6), "rstd"], writes=["nmr"])
            deferred = []
            for c in range(KC):
                b = c % 2
                S.op("dve", TT(tt[b][:, :n], xs[:, c, lo:lo + n], rstd[:, :n], ALU.mult),
                     reads=[("x", c, lo), "rstd"], writes=[("tt", b)])
                if s < 2:
                    S.op("dve", TT(xs[:, c, lo:lo + n], tt[b][:, :n], nmr[:, :n], ALU.add), reads=[("tt", b), "nmr"], writes=[("x", c, lo)])
                    S.op("act", ACT(us[:, c, lo:lo + n], xs[:, c, lo:lo + n], AF.Identity,
                                    bias=der2[:, 32 + s * 16 + c:32 + s * 16 + c + 1], scale=der2[:, s * 16 + c:s * 16 + c + 1]),
                         reads=[("x", c, lo), ("der2", s)], writes=[("u", c, lo)])
                    deferred.append(c)
                else:
                    S.op("dve", TT(tt[b][:, :n], tt[b][:, :n], nmr[:, :n], ALU.add), reads=[("tt", b), "nmr"], writes=[("tt", b)])
                    S.op("act", ACT(xs[:, c, lo:lo + n], tt[b][:, :n], AF.Identity, bias=lba(s, c), scale=lga(s, c)),
                         reads=[("tt", b), ("der", "ln")], writes=[("x", c, lo)])
                    if c % 4 == 3:
                        col = p * OWN + lo - 128
                        c4 = c // 4
                        out_ops[("out", c4)] = S.op(
                            "sp", DMA(outT[c4 * 512:(c4 + 1) * 512, col:col + n].rearrange("(c p) t -> p c t", p=128),
                                      xs[:, c4 * 4:c4 * 4 + 4, lo:lo + n]),
                            reads=[("x", cc_, lo) for cc_ in range(c4 * 4, c4 * 4 + 4)], dma_key=("out", c4))
            for c in deferred:
                S.op("act", ACT(xs[:, c, lo:lo + n], xs[:, c, lo:lo + n], AF.Identity, bias=lba(s, c), scale=lga(s, c)),
                     reads=[("x", c, lo), ("der", "ln")], writes=[("x", c, lo)])
                if DEBUG and lo >= 128 and c % 4 == 3:
                    dd = dbg1 if s == 0 else dbg2
                    col = p * OWN + lo - 128
                    c4 = c // 4
                    out_ops[("dbg", s, c4)] = S.op(
                        "sp", DMA(dd[c4 * 512:(c4 + 1) * 512, col:col + n].rearrange("(c p) t -> p c t", p=128),
                                  xs[:, c4 * 4:c4 * 4 + 4, lo:lo + n]),
                        reads=[("x", cc_, lo) for cc_ in range(c4 * 4, c4 * 4 + 4)], dma_key=("dbg", s, c4))

        def ln_temps(A):
            xb = [A.bf16("xb", 512) for _ in range(2)]
            sq = [A.bf16("sq", 512) for _ in range(2)]
            t1 = A.f32("t1", 512)
            rstd = A.f32("rstd", 512)
            nmr = A.f32("nmr", 512)
            tt = [A.f32("tt", 512) for _ in range(2)]
            return (xb, sq, t1, rstd, nmr, tt)

        def ffn(s, fi, tcs, p, inter=None, after_ln=None):
            A.mark()
            slt = [A.f32("slt", 512) for _ in range(2)]
            sub = A.sub(3584)
            wd = [A.bf16("wd", D) for _ in range(8)]
            wgu = [A.bf16("wgu", 16 * 256).rearrange("p (k j) -> p k j", k=16) for _ in range(2)]
            hb = [A.bf16("h", 4 * TP).rearrange("p (f t) -> p f t", f=4) for _ in range(2)]
            acc = [A.f32("acc", 512) for _ in range(2)]
            groups = [list(range(g * 4, min(NFC, g * 4 + 4))) for g in range(11)]
            cnt = [0, 0]
            LT = [None]

            def up(g):
                fcs = groups[g]
                for i0 in range(0, len(fcs), 2):
                    pair = fcs[i0:i0 + 2]
                    for fc in pair:
                        sl = fc % 2
                        S.op("pool", DMA(wgu[sl].rearrange("p k j -> p (k j)"), wgu_d[fi][fc]), writes=[("wgu", sl)],
                             dma_key=("wgu", sl))
                    for (lo, n) in tcs:
                        for fc in pair:
                            sl = fc % 2
                            f_i = fc - fcs[0]
                            b = cnt[0] % 2
                            cnt[0] += 1
                            pa, pb = 2 * b, 2 * b + 1
                            for kc in range(KC):
                                S.op("pe", MM(banks[pa][:, :n], wgu[sl][:, kc, 0:128], us[:, kc, lo:lo + n], kc == 0, kc == KC - 1),
                                     reads=[("wgu", sl), ("u", kc, lo)], writes=[PB(pa)])
                            for kc in range(KC):
                                S.op("pe", MM(banks[pb][:, :n], wgu[sl][:, kc, 128:256], us[:, kc, lo:lo + n], kc == 0, kc == KC - 1),
                                     reads=[("wgu", sl), ("u", kc, lo)], writes=[PB(pb)])
                            S.op("act", ACT(slt[b][:, :n], banks[pa][:, :n], AF.Silu), reads=[PB(pa)], writes=[("slt", b)])
                            S.op("dve", TT(hb[g % 2][:, f_i, lo:lo + n], slt[b][:, :n], banks[pb][:, :n], ALU.mult),
                                 reads=[("slt", b), PB(pb)], writes=[("h", g % 2, f_i, lo)])
                            if mgen[0] is not None:
                                if next(mgen[0], "done") == "done":
                                    mgen[0] = None
                for fc in fcs:
                    S.op("pool", DMA(wd[fc % 8], wd_d[fi][fc]), writes=[("wd", fc % 8)], dma_key=("wd", fc % 8))

            def down(g, last):
                fcs = groups[g]
                order = [(dc, t) for t in tcs for dc in range(KC)] if last else [(dc, t) for dc in range(KC) for t in tcs]
                for dc, (lo, n) in order:
                    if last:
                        bk = 4 + cnt[1] % 2
                    elif inter is not None:
                        bk = (4, 5, 7)[cnt[1] % 3]
                    else:
                        bk = 4 + cnt[1] % 4
                    cnt[1] += 1
                    for f_i, fc in enumerate(fcs):
                        S.op("pe", MM(banks[bk][:, :n], wd[fc % 8][:, dc * 128:(dc + 1) * 128], hb[g % 2][:, f_i, lo:lo + n],
                                      f_i == 0, f_i == len(fcs) - 1),
                             reads=[("wd", fc % 8), ("h", g % 2, f_i, lo)], writes=[PB(bk)])
                    if POOL_ACC and (not last) and cnt[1] % 3 == 2:
                        ab = (cnt[1] // 3) % 2
                        S.op("act", ACT(acc[ab][:, :n], banks[bk][:, :n], AF.Identity, scale=gt(s, dc)),
                             reads=[PB(bk), ("der", "gt", s)], writes=[("acc", ab)])
                        S.op("pool", lambda e, o_=xs[:, dc, lo:lo + n], a_=acc[ab][:, :n]: e.tensor_tensor(o_, o_, a_, ALU.add),
                             reads=[("acc", ab), ("x", dc, lo)], writes=[("x", dc, lo)])
                    else:
                        S.op("dve", STT(xs[:, dc, lo:lo + n], banks[bk][:, :n], gt(s, dc), xs[:, dc, lo:lo + n], ALU.mult, ALU.add),
                             reads=[PB(bk), ("x", dc, lo), ("der", "gt", s)], writes=[("x", dc, lo)])
                    if last and dc == KC - 1:
                        layer_norm(s, lo, n, LT[0], p)
                        if after_ln is not None:
                            after_ln(lo, n)

            mgen = [None]
            if inter is not None:
                mgen[0] = mod_gen(inter, mod_alloc(sub), (6, 6), 7)
            up(0)
            for g in range(1, 11):
                up(g)
                if g == 10:
                    assert mgen[0] is None or next(mgen[0], "done") == "done", "mod interleave not finished"
                    sub.reset()
                    LT[0] = ln_temps(sub)
                down(g - 1, False)
            down(10, True)
            sub.reset()
            A.release()

        def mixing(p, tcs_all, own):
            pass
            A.mark()
            base0 = A.top
            oT = A.top_bf16("oT", 8 * OWN).rearrange("p (c t) -> p c t", c=8)
            mixed = A.top_bf16("mixed", 8 * OWN).rearrange("p (c t) -> p c t", c=8)
            A.mark()
            r1 = [A.f32("r1", 512) for _ in range(2)]
            qz = [A.bf16("qz", 2 * OWN).rearrange("p (h t) -> p h t", h=2) for _ in range(2)]
            PT = [A.bf16("PT", 512).rearrange("p (k h q) -> p k h q", k=2, h=2) for _ in range(2)]
            dn = [A.f32("dn", 256) for _ in range(2)]
            A.pad_to(base0 + 4608)
            r2 = [A.f32("r2", 512) for _ in range(2)]
            qt = [A.bf16("qt", 512) for _ in range(2)]
            kT = A.bf16("kT", 2 * TP).rearrange("p (c t) -> p c t", c=2)
            V = A.bf16("V", 9 * 256).rearrange("p (b j) -> p b j", b=9)
            win = [A.bf16("win", 16 * 128).rearrange("p (k j) -> p k j", k=16) for _ in range(2)]
            wv = A.bf16("wv", 16 * 256).rearrange("p (k j) -> p k j", k=16)
            rC = A.f32("rC", TP)
            rS = A.f32("rS", TP)
            c0 = 0 if p == 0 else 1152
            s0 = 0 if p == 0 else 128
            nload = TP - s0
            wcnt = [0]
            pcnt = [0]

            def load_win(idx):
                sl = wcnt[0] % 2
                wcnt[0] += 1
                S.op("pool", DMA(win[sl].rearrange("p k j -> p (k j)"), win_d[idx]), writes=[("win", sl)], dma_key=("win", sl))
                return sl

            ksl = [load_win(8), load_win(9)]
            S.op("pool", DMA(wv.rearrange("p k j -> p (k j)"), wv_d), writes=["wv"], dma_key="wv")
            S.op("sp", DMA(rC[:, s0:TP], ropeC_d[:, c0:c0 + nload]), writes=["rC"], dma_key="rC")
            S.op("sp", DMA(rS[:, s0:TP], ropeS_d[:, c0:c0 + nload]), writes=["rS"], dma_key="rS")
            if p == 1:
                S.op("dve", CP(kT[:, :, 0:128], kcarry), reads=["kcarry"], writes=[("kT", 0, 0), ("kT", 1, 0)])
                S.op("dve", CP(V[:, 0, :], vcarry), reads=["vcarry"], writes=[("V", 0)])

            def proj_rope(sl, bcol, lo, n):
                b = pcnt[0] % 2
                pcnt[0] += 1
                for kc in range(KC):
                    S.op("pe", MM(banks[b][:, :n], win[sl][:, kc, :], us[:, kc, lo:lo + n], kc == 0, kc == KC - 1),
                         reads=[("win", sl), ("u", kc, lo)], writes=[PB(b)])
                S.op("act", ACT(qt[b][:, :n], banks[b][:, :n], AF.Identity, bias=biasfm[:, bcol:bcol + 1]),
                     reads=[PB(b), "biasfm"], writes=[("qt", b)])
                S.op("pe", MM(banks[2 + b][:, :n], cm[:, PERM, :], qt[b][:, :n], True, True), reads=[("qt", b), "cm"], writes=[PB(2 + b)])
                S.op("dve", TT(r1[b][:, :n], qt[b][:, :n], rC[:, lo:lo + n], ALU.mult), reads=[("qt", b), "rC"], writes=[("r1", b)])
                S.op("dve", TT(r2[b][:, :n], banks[2 + b][:, :n], rS[:, lo:lo + n], ALU.mult), reads=[PB(2 + b), "rS"], writes=[("r2", b)])
                return b

            for (lo, n) in (tcs_all if p == 0 else own):
                for j in range(2):
                    b = proj_rope(ksl[j], 16 + j, lo, n)
                    tags = [("kT", j, blk) for blk in range(lo // 128, (lo + n) // 128)]
                    S.op("dve", TT(kT[:, j, lo:lo + n], r1[b][:, :n], r2[b][:, :n], ALU.add), reads=[("r1", b), ("r2", b)], writes=tags)
                for blk in range(lo // 128, (lo + n) // 128):
                    b = pcnt[0] % 2
                    pcnt[0] += 1
                    for kc in range(KC):
                        S.op("pe", MM(banks[b][:, 0:256], us[:, kc, blk * 128:(blk + 1) * 128], wv[:, kc, :], kc == 0, kc == KC - 1),
                             reads=["wv", ("u", kc, lo)], writes=[PB(b)])
                    S.op("dve", TT(V[:, blk, :], banks[b][:, 0:256], bvbc, ALU.add), reads=[PB(b), "bvbc"], writes=[("V", blk)])
            if p == 0:
                S.op("dve", CP(kcarry, kT[:, :, 1024:1152]), reads=[("kT", 0, 8), ("kT", 1, 8)], writes=["kcarry"])
                S.op("dve", CP(vcarry, V[:, 8, :]), reads=[("V", 8)], writes=["vcarry"])
            for b in range(2):
                S.op("dve", MEMSET(qz[b][64:128, 0, :], 0.0), writes=[("qz", b)])
                S.op("dve", MEMSET(qz[b][0:64, 1, :], 0.0), writes=[("qz", b)])
            def qproj_gen(i):
                sl = load_win(i)
                qb = i % 2
                for (lo, n) in own:
                    b = pcnt[0] % 2
                    pcnt[0] += 1
                    for kc in range(KC):
                        S.op("pe", MM(banks[b][:, :n], win[sl][:, kc, :], us[:, kc, lo:lo + n], kc == 0, kc == KC - 1),
                             reads=[("win", sl), ("u", kc, lo)], writes=[PB(b)])
                        if kc % 8 == 7:
                            yield
                    S.op("act", ACT(qt[b][:, :n], banks[b][:, :n], AF.Identity, bias=biasfm[:, 8 + i:9 + i]),
                         reads=[PB(b), "biasfm"], writes=[("qt", b)])
                    S.op("pe", MM(banks[2 + b][:, :n], cm[:, PERM, :], qt[b][:, :n], True, True), reads=[("qt", b), "cm"], writes=[PB(2 + b)])
                    S.op("dve", TT(r1[b][:, :n], qt[b][:, :n], rC[:, lo:lo + n], ALU.mult), reads=[("qt", b), "rC"], writes=[("r1", b)])
                    S.op("dve", TT(r2[b][:, :n], banks[2 + b][:, :n], rS[:, lo:lo + n], ALU.mult), reads=[PB(2 + b), "rS"], writes=[("r2", b)])
                    o0 = lo - 128
                    S.op("dve", TT(qz[qb][0:64, 0, o0:o0 + n], r1[b][0:64, :n], r2[b][0:64, :n], ALU.add),
                         reads=[("r1", b), ("r2", b)], writes=[("qz", qb)])
                    S.op("dve", TT(qz[qb][64:128, 1, o0:o0 + n], r1[b][64:128, :n], r2[b][64:128, :n], ALU.add),
                         reads=[("r1", b), ("r2", b)], writes=[("qz", qb)])
                    yield

            for _ in qproj_gen(0):
                pass
            acnt = [0]
            for i in range(8):
                j = i // 4
                t_ = i % 4
                ha, hb_ = 8 * j + t_, 8 * j + 4 + t_
                qb = i % 2

                def scores(nb, it):
                    sb = 4 + it % 2
                    for kb_i, kblk in enumerate((nb - 1, nb)):
                        if kb_i == 1:
                            mk = MASKD
                        else:
                            mk = MASKP0 if (p == 0 and nb == 1) else MASKP
                        S.op("pe", MM(banks[sb][:, kb_i * 256:kb_i * 256 + 256], cm[:, IDENT, :],
                                      cm[:, mk:mk + 2, :].rearrange("p m j -> p (m j)"), True, False),
                             reads=["cm"], writes=[PB(sb)])
                        for h_i in range(2):
                            c_lo = kb_i * 256 + h_i * 128
                            S.op("pe", MM(banks[sb][:, c_lo:c_lo + 128], kT[:, j, kblk * 128:(kblk + 1) * 128],
                                          qz[qb][:, h_i, (nb - 1) * 128:nb * 128], False, True),
                                 reads=[("kT", j, kblk), ("qz", qb)], writes=[PB(sb)])
                    pt = PT[it % 2]
                    bview = banks[sb][:, :].rearrange("p (k h q) -> p k h q", k=2, h=2)
                    for h_i, hd in enumerate((ha, hb_)):
                        S.op("act", ACT(pt[:, :, h_i, :], bview[:, :, h_i, :], AF.Exp, bias=nsink[:, hd:hd + 1], scale=0.125),
                             reads=[PB(sb), "nsink"], writes=[("PT", it % 2)])

                def pv(nb, it):
                    bk = 6 + it % 2
                    pt = PT[it % 2]
                    for kb_i, kblk in enumerate((nb - 1, nb)):
                        S.op("pe", MM(banks[bk][:, 0:256], V[:, kblk, j * 128:(j + 1) * 128],
                                      pt[:, kb_i, :, :].rearrange("p h q -> p (h q)"), kb_i == 0, kb_i == 1),
                             reads=[("V", kblk), ("PT", it % 2)], writes=[PB(bk)])
                    for kb_i, kblk in enumerate((nb - 1, nb)):
                        S.op("pe", MM(banks[bk][:, 256:512], cm[:, ONES, :],
                                      pt[:, kb_i, :, :].rearrange("p h q -> p (h q)"), kb_i == 0, kb_i == 1),
                             reads=["cm", ("PT", it % 2)], writes=[PB(bk)])
                    d_ = dn[it % 2]
                    S.op("act", ACT(d_, banks[bk][:, 256:512], AF.Identity, bias=onet[:, 0:1]), reads=[PB(bk), "onet"], writes=[("dn", it % 2)])
                    S.op("dve", RCP(d_, d_), reads=[("dn", it % 2)], writes=[("dn", it % 2)])
                    q0 = (nb - 1) * 128
                    S.op("dve", TT(oT[0:64, i, q0:q0 + 128], banks[bk][0:64, 0:128], d_[0:64, 0:128], ALU.mult),
                         reads=[PB(bk), ("dn", it % 2)], writes=[("oT", i)])
                    S.op("dve", TT(oT[64:128, i, q0:q0 + 128], banks[bk][64:128, 128:256], d_[64:128, 128:256], ALU.mult),
                         reads=[PB(bk), ("dn", it % 2)], writes=[("oT", i)])

                its = []
                for nb in range(1, 9):
                    its.append((nb, acnt[0]))
                    acnt[0] += 1
                gnext = qproj_gen(i + 1) if i + 1 < 8 else None
                for idx, (nb, it) in enumerate(its):
                    scores(nb, it)
                    if idx >= 1:
                        pv(*its[idx - 1])
                    if gnext is not None:
                        next(gnext, None)
                pv(*its[-1])
                if gnext is not None:
                    for _ in gnext:
                        pass
            A.release()
            pass
            A.mark()
            pooled = [A.bf16("pooled", 2 * OWN).rearrange("p (c t) -> p c t", c=2) for _ in range(2)]
            ptm = [A.f32("ptm", 528) for _ in range(2)]
            stm = [A.f32("stm", 528) for _ in range(2)]
            st2 = [A.f32("st2", 528) for _ in range(2)]
            t16 = A.f32("t16", 16)
            A.pad_to(base0 + 8192)
            win = [A.bf16("win", 16 * 128).rearrange("p (k j) -> p k j", k=16) for _ in range(2)]
            wpl = A.bf16("wpl", 4 * 2 * 256).rearrange("p (g c d) -> p g c d", g=4, c=2)
            S.op("pool", DMA(wpl.rearrange("p g c d -> p (g c d)"), wpool_d), writes=["wpl"], dma_key="wpl")
            pcnt = [0]
            wcnt = [0]
            for pc in range(8):
                g = pc // 2
                cc = pc % 2
                w_ = 2 ** (g + 1)
                sl = wcnt[0] % 2
                wcnt[0] += 1
                S.op("pool", DMA(win[sl].rearrange("p k j -> p (k j)"), win_d[10 + pc]), writes=[("win", sl)], dma_key=("win", sl))
                pbuf = pooled[g % 2]
                for (lo, n) in (tcs_all if p == 0 else own):
                    b = pcnt[0] % 2
                    pcnt[0] += 1
                    for kc in range(KC):
                        S.op("pe", MM(banks[b][:, :n], win[sl][:, kc, :], us[:, kc, lo:lo + n], kc == 0, kc == KC - 1),
                             reads=[("win", sl), ("u", kc, lo)], writes=[PB(b)])
                    if lo == 0:
                        S.op("dve", TS(pcarry[:, pc, :], banks[b][:, 112:128], biasfm[:, pc:pc + 1], hv[:, 0:1], ALU.add, ALU.mult),
                             reads=[PB(b), "biasfm", "hv"], writes=[("pcarry", pc)])
                        continue
                    pt_ = ptm[b]
                    S.op("act", ACT(pt_[:, 16:16 + n], banks[b][:, :n], AF.Identity, bias=biasfm[:, pc:pc + 1]),
                         reads=[PB(b), "biasfm"], writes=[("ptm", b)])
                    S.op("dve", CP(pt_[:, 0:16], pcarry[:, pc, :]), reads=[("pcarry", pc)], writes=[("ptm", b)])
                    S.op("dve", CP(pcarry[:, pc, :], pt_[:, n:n + 16]), reads=[("ptm", b)], writes=[("pcarry", pc)])
                    cur = pt_
                    curtag = ("ptm", b)
                    tmps = [(stm[b], ("stm", b)), (st2[b], ("st2", b))]
                    for k in range(g + 1):
                        sh = 1 << k
                        lo_k = 2 * sh - 1
                        dst, dtag = tmps[k % 2]
                        S.op("dve", TT(dst[:, lo_k:16 + n], cur[:, lo_k:16 + n], cur[:, lo_k - sh:16 + n - sh], ALU.add),
                             reads=[curtag], writes=[dtag])
                        cur, curtag = dst, dtag
                    o0 = lo - 128
                    S.op("dve", STT(pbuf[:, cc, o0:o0 + n], cur[:, 16:16 + n], 1.0 / w_, pt_[:, 16:16 + n], ALU.mult, ALU.subtract),
                         reads=[curtag, ("ptm", b)], writes=[("pooled", g % 2, cc)])
                    if p == 0 and lo == 128:
                        S.op("dve", TT(t16, cur[:, 16:32], invc[:, g * 16:(g + 1) * 16], ALU.mult), reads=[curtag, "invc"], writes=["t16"])
                        S.op("dve", TT(pbuf[:, cc, 0:16], t16, pt_[:, 16:32], ALU.subtract), reads=["t16", ("ptm", b)],
                             writes=[("pooled", g % 2, cc)])
                if cc == 1:
                    for dcc in range(2):
                        for (lo, n) in own:
                            b = 2 + pcnt[0] % 2
                            pcnt[0] += 1
                            o0 = lo - 128
                            for c2 in range(2):
                                S.op("pe", MM(banks[b][:, :n], wpl[:, g, c2, dcc * 128:(dcc + 1) * 128], pbuf[:, c2, o0:o0 + n], c2 == 0, c2 == 1),
                                     reads=["wpl", ("pooled", g % 2, c2)], writes=[PB(b)])
                            mc = 2 * g + dcc
                            S.op("act", ACT(mixed[:, mc, o0:o0 + n], banks[b][:, :n], AF.Identity, scale=biasfm[:, 52 + mc:53 + mc]),
                                 reads=[PB(b), "biasfm"], writes=[("mixed", mc)])
            A.release()
            for (lo, n) in own:
                o0 = lo - 128
                pass
                A.mark()
                merged = A.bf16("merged", 16 * 512).rearrange("p (c t) -> p c t", c=16)
                A.mark()
                ring = [A.bf16("ring", 16 * 128).rearrange("p (k j) -> p k j", k=16) for _ in range(6)]
                sab = [[A.f32("sa", 512), A.f32("sb", 512)] for _ in range(2)]
                for dc in range(KC):
                    par = dc % 2
                    rs = [ring[par * 3 + r] for r in range(3)]
                    for r in range(3):
                        S.op("pool", DMA(rs[r].rearrange("p k j -> p (k j)"), wmix_d[dc, r]), writes=[("ring", par * 3 + r)],
                             dma_key=("ring", par * 3 + r))
                    b0 = par * 4
                    for kc in range(8):
                        S.op("pe", MM(banks[b0][:, :n], rs[0][:, kc, :], mixed[:, kc, o0:o0 + n], kc == 0, kc == 7),
                             reads=[("ring", par * 3), ("mixed", kc)], writes=[PB(b0)])
                    for kc in range(8):
                        S.op("pe", MM(banks[b0 + 1][:, :n], rs[0][:, 8 + kc, :], oT[:, kc, o0:o0 + n], kc == 0, kc == 7),
                             reads=[("ring", par * 3), ("oT", kc)], writes=[PB(b0 + 1)])
                    for r in (1, 2):
                        for kc in range(KC):
                            S.op("pe", MM(banks[b0 + 1 + r][:, :n], rs[r][:, kc, :], us[:, kc, lo:lo + n], kc == 0, kc == KC - 1),
                                 reads=[("ring", par * 3 + r), ("u", kc, lo)], writes=[PB(b0 + 1 + r)])
                    sa, sb_ = sab[par]
                    S.op("act", ACT(sa[:, :n], banks[b0 + 2][:, :n], AF.Sigmoid, bias=biasfm[:, 20 + dc:21 + dc]),
                         reads=[PB(b0 + 2), "biasfm"], writes=[("sa", par)])
                    S.op("dve", TT(sa[:, :n], sa[:, :n], banks[b0][:, :n], ALU.mult), reads=[("sa", par), PB(b0)], writes=[("sa", par)])
                    S.op("act", ACT(sb_[:, :n], banks[b0 + 3][:, :n], AF.Sigmoid, bias=biasfm[:, 36 + dc:37 + dc]),
                         reads=[PB(b0 + 3), "biasfm"], writes=[("sb", par)])
                    S.op("dve", TT(sb_[:, :n], sb_[:, :n], banks[b0 + 1][:, :n], ALU.mult), reads=[("sb", par), PB(b0 + 1)], writes=[("sb", par)])
                    S.op("dve", TT(merged[:, dc, :n], sa[:, :n], sb_[:, :n], ALU.add), reads=[("sa", par), ("sb", par)], writes=[("merged", dc)])
                A.release()
                pass
                A.mark()
                wo = [A.bf16("wo", 16 * 128).rearrange("p (k j) -> p k j", k=16) for _ in range(4)]
                if (lo, n) != own[-1]:
                    A.pad_to(base0 + 10240)
                LT = ln_temps(A)
                for d2 in range(KC):
                    sl = d2 % 4
                    S.op("pool", DMA(wo[sl].rearrange("p k j -> p (k j)"), wo_d[d2]), writes=[("wo", sl)], dma_key=("wo", sl))
                    bk = d2 % 4
                    for kc in range(KC):
                        S.op("pe", MM(banks[bk][:, :n], wo[sl][:, kc, :], merged[:, kc, :n], kc == 0, kc == KC - 1),
                             reads=[("wo", sl), ("merged", kc)], writes=[PB(bk)])
                    S.op("dve", STT(xs[:, d2, lo:lo + n], banks[bk][:, :n], gt(1, d2), xs[:, d2, lo:lo + n], ALU.mult, ALU.add),
                         reads=[PB(bk), ("x", d2, lo), ("der", "gt", 1)], writes=[("x", d2, lo)])
                layer_norm(1, lo, n, LT, p)
                A.release()
                A.release()
            A.release_top()
            A.release()

        epst = A.f32("epst", 1)
        onet = A.f32("onet", 1)
        S.op("dve", MEMSET(epst, LN_EPS), writes=["epst"])
        S.op("dve", MEMSET(onet, 1.0), writes=["onet"])

        inter0 = list(range(8, 36))
        own = [(128, 512), (640, 512)]

        def prologue(p, lo, n):
            col = lo if p == 0 else 1024 + lo
            for c4 in range(4):
                S.op("sp", DMA(xs[:, c4 * 4:c4 * 4 + 4, lo:lo + n],
                               xT[c4 * 512:(c4 + 1) * 512, col:col + n].rearrange("(c p) t -> p c t", p=128)),
                     writes=[("x", c, lo) for c in range(c4 * 4, c4 * 4 + 4)], dma_key=("x", c4, lo))
            for c in range(KC):
                S.op("dve", TS(us[:, c, lo:lo + n], xs[:, c, lo:lo + n], s1p(0, c), shift(0, c), ALU.mult, ALU.add),
                     reads=[("x", c, lo), ("der", "s1p", 0), ("modfm", 0)], writes=[("u", c, lo)])
                S.op("act", ACT(xs[:, c, lo:lo + n], xs[:, c, lo:lo + n], AF.Identity, scale=ALPHA),
                     reads=[("x", c, lo)], writes=[("x", c, lo)])

        for (lo, n) in [(0, 128)] + own:
            prologue(0, lo, n)
        ffn(0, 0, [(0, 128)] + own, 0, inter=inter0)
        mixing(0, [(0, 128)] + own, own)
        ffn(2, 1, own, 0, after_ln=lambda lo, n: prologue(1, lo, n))
        ffn(0, 0, own, 1)
        mixing(1, own, own)
        ffn(2, 1, own, 1)

        S.emit(final_waits=list(out_ops.values()))
    return nc


def _shared_layout(inp):
    f = np.float32
    sh = {}
    w_ada = inp["w_ada"][0]
    sh["wada"] = np.ascontiguousarray(
        w_ada.reshape(4, 4, 128, 36, 512).transpose(3, 0, 2, 1, 4).reshape(144, 128, 4 * 512))
    sh["bada"] = np.ascontiguousarray(inp["b_ada"][0].reshape(9, 16, 128).transpose(2, 0, 1).reshape(128, 144))
    g = inp["ln_g"][0].reshape(3, 16, 128).transpose(2, 0, 1).reshape(128, 48)
    b = inp["ln_b"][0].reshape(3, 16, 128).transpose(2, 0, 1).reshape(128, 48)
    sh["lngb"] = np.ascontiguousarray(np.concatenate([g, b], axis=1))
    for nm, kin, kout in (("1", "w_ffn1_in", "w_ffn1_out"), ("2", "w_ffn2_in", "w_ffn2_out")):
        wi = inp[kin][0].reshape(16, 128, 2, NFC, 128)
        sh["wgu" + nm] = np.ascontiguousarray(wi.transpose(3, 1, 0, 2, 4).reshape(NFC, 128, 16 * 256))
        sh["wd" + nm] = np.ascontiguousarray(inp[kout][0].reshape(NFC, 128, D))
    w_in = inp["w_in"][0]
    b_in = inp["b_in"][0]
    qcols = []
    for i in range(8):
        j, t = i // 4, i % 4
        for h in (8 * j + t, 8 * j + 4 + t):
            qcols.extend(range(1024 + h * 64, 1024 + (h + 1) * 64))
    qcols = np.array(qcols)
    cols = np.concatenate([qcols, np.arange(2048, 2304), np.arange(0, 1024)])
    wsel = w_in[:, cols].reshape(16, 128, 18, 128)
    sh["win"] = np.ascontiguousarray(wsel.transpose(2, 1, 0, 3).reshape(18, 128, 16 * 128))
    sh["wv"] = np.ascontiguousarray(w_in[:, 2304:2560].reshape(16, 128, 256).transpose(1, 0, 2).reshape(128, 16 * 256))
    wa = inp["w_branch_a"][0].reshape(8, 128, 16, 128)
    wb_rows = qcols - 1024
    wb = inp["w_branch_b"][0][wb_rows].reshape(8, 128, 16, 128)
    wab = np.concatenate([wa, wb], axis=0).transpose(2, 1, 0, 3).reshape(16, 128, 16 * 128)
    wga = w_in[:, 2560:4608].reshape(16, 128, 16, 128).transpose(2, 1, 0, 3).reshape(16, 128, 16 * 128)
    wgb = w_in[:, 4608:6656].reshape(16, 128, 16, 128).transpose(2, 1, 0, 3).reshape(16, 128, 16 * 128)
    sh["wmix"] = np.ascontiguousarray(np.stack([wab, wga, wgb], axis=1))
    sh["wo"] = np.ascontiguousarray(inp["w_out"][0].reshape(16, 128, 16, 128).transpose(2, 1, 0, 3).reshape(16, 128, 16 * 128))
    sh["wpool"] = np.ascontiguousarray(inp["w_pool"][0].reshape(4, 2, 128, 256).transpose(2, 0, 1, 3).reshape(128, 4 * 2 * 256))
    bias = np.zeros((128, 64), f)
    bias[:, 0:8] = b_in[0:1024].reshape(8, 128).T
    bias[:, 8:16] = b_in[qcols].reshape(8, 128).T
    bias[:, 16:18] = b_in[2048:2304].reshape(2, 128).T
    bias[:, 20:36] = b_in[2560:4608].reshape(16, 128).T
    bias[:, 36:52] = b_in[4608:6656].reshape(16, 128).T
    bias[:, 52:60] = inp["pool_scale"][0].reshape(8, 128).T
    sh["biasfm"] = bias
    sh["bvbc"] = np.ascontiguousarray(np.broadcast_to(b_in[2304:2560][None, :], (128, 256))).astype(f)
    sh["sinkbc"] = np.ascontiguousarray(np.broadcast_to(inp["sinks"][0][None, :], (128, 16))).astype(f)
    return sh


def _core_consts(half):
    f = np.float32
    inv_freq = (500000.0 ** (-np.arange(0, 16, 2, dtype=f) / 16)).astype(f)
    pos = np.concatenate([np.arange(half * 2048 - 128, half * 2048), np.arange(half * 2048, half * 2048 + 2048)]).astype(f)
    pos = np.maximum(pos, 0).astype(f)
    ang = (pos[:, None] * inv_freq[None, :]).astype(f)
    cos, sin = np.cos(ang).astype(f), np.sin(ang).astype(f)
    rC = np.ones((128, 2176), f)
    rS = np.zeros((128, 2176), f)
    for hh in range(2):
        for d in range(16):
            pp = hh * 64 + d
            rC[pp] = cos[:, d % 8]
            rS[pp] = -sin[:, d % 8] if d < 8 else sin[:, d % 8]
    cm = np.zeros((10, 128, 128), f)
    cm[0] = np.eye(128, dtype=f)
    jj = np.arange(128)[:, None]
    ii = np.arange(128)[None, :]
    cm[1] = cm[2] = np.where(jj > ii, 0.0, NEG)
    cm[3] = cm[4] = np.where(jj <= ii, 0.0, NEG)
    cm[5] = cm[6] = cm[1] if half == 1 else np.full((128, 128), NEG, f)
    for m in range(128):
        d = m % 64
        if d < 16:
            k = m + 8 if d < 8 else m - 8
            cm[7][k, m] = 1.0
    cm[8] = 1.0 / 2048.0
    cm[9] = 1.0
    cmat = np.ascontiguousarray(cm.transpose(1, 0, 2).reshape(128, 10 * 128))
    invc = np.zeros((128, 64), f)
    for g in range(4):
        w = 2 ** (g + 1)
        t = np.arange(16)
        v = (1.0 / np.minimum(t + 1, w)) if half == 0 else np.full(16, 1.0 / w)
        invc[:, g * 16:(g + 1) * 16] = v.astype(f)[None, :]
    hv = np.full((128, 1), float(half), f)
    return {"ropeC": rC, "ropeS": rS, "cmat": cmat, "invcnt": invc, "hv": hv}


_CACHE = {}


def kernel(**inputs):
    inp = {k: np.asarray(v) for k, v in inputs.items()}
    if "nc" not in _CACHE:
        _CACHE["nc"] = build_program()
    nc = _CACHE["nc"]
    sh = _shared_layout(inp)
    x = inp["x"]
    c = inp["c"]
    in_maps = []
    for core in range(8):
        b, half = core // 2, core % 2
        m = dict(sh)
        xt = np.zeros((D, 2176), np.float32)
        if half == 1:
            xt[:, 0:128] = x[b, 2048 - 128:2048].T
        xt[:, 128:] = x[b, half * 2048:(half + 1) * 2048].T
        m["xT"] = xt
        m["cfm"] = np.ascontiguousarray(c[b].reshape(16, 128).T)
        m.update(_core_consts(half))
        in_maps.append(m)
    res = run_bass_kernel_spmd(nc, in_maps, core_ids=list(range(8)))
    _CACHE["res"] = res
    out = np.empty((4, 4096, D), np.float32)
    for core in range(8):
        b, half = core // 2, core % 2
        out[b, half * 2048:(half + 1) * 2048] = res.results[core]["outT"].T
    return out
```
